# Optimizing a Trainium2 kernel written in Bass

```python
import jax, jax.numpy as jnp
from jax import lax
import numpy as np

D_MODEL = 1024
BATCH = 4
SEQ = 4096
DEPTH = 4
DEC_BATCH = 16
DEC_SEQ = 2048
PAST_LEN = 128

HEAD_DIM = 64
N_HEADS_NA = 6
N_HEADS_WIN = 6
N_KV_WIN = 2
N_HEADS_MEM = 4
D_NA = N_HEADS_NA * HEAD_DIM
D_WIN = N_HEADS_WIN * HEAD_DIM
D_KV_WIN = N_KV_WIN * HEAD_DIM
D_MEM = N_HEADS_MEM * HEAD_DIM
D_MIX = D_NA + D_WIN + D_MEM
D_IN = 4 * D_NA + 2 * D_WIN + 2 * D_KV_WIN + 2 * D_MEM
N_MEM = 256
GRID_W = 64
NA_ROWS_MAX = 8
NA_COLS = 16
NA_QCOLS = 16
NA_KCOLS = 32
WINDOW = 128
WIN_BLOCK = 128
RMS_EPS = 1e-6
NEG_INF = -1e30

kernel_name = "hymba_natten_swa_memory_encoder"


def rms_norm(x, g):
    xf = x.astype(jnp.float32)
    y = xf * lax.rsqrt(jnp.mean(xf * xf, axis=-1, keepdims=True) + RMS_EPS)
    return (y * g.astype(jnp.float32)).astype(x.dtype)


def alibi_slopes(n_heads):
    return 2.0 ** (-8.0 * jnp.arange(1, n_heads + 1, dtype=jnp.float32) / n_heads)


def neighbourhood_attention(q, k, v, rpb):
    b, t, h, d = q.shape
    rows = t // GRID_W
    kh = min(NA_ROWS_MAX, rows)
    qg = q.reshape(b, rows, GRID_W, h, d)
    kg = k.reshape(b, rows, GRID_W, h, d)
    vg = v.reshape(b, rows, GRID_W, h, d)
    r = jnp.arange(rows)
    r0 = jnp.clip(r - kh // 2, 0, rows - kh)
    row_idx = r0[:, None] + jnp.arange(kh)[None, :]
    rel_row = row_idx - r[:, None] + (NA_ROWS_MAX - 1)
    outs = []
    for c0 in range(0, GRID_W, NA_QCOLS):
        kc0 = min(max(c0 - NA_COLS // 2, 0), GRID_W - NA_KCOLS)
        qc = c0 + jnp.arange(NA_QCOLS)
        kc = kc0 + jnp.arange(NA_KCOLS)
        cs = jnp.clip(qc - NA_COLS // 2, 0, GRID_W - NA_COLS)
        col_mask = (kc[None, :] >= cs[:, None]) & (kc[None, :] < cs[:, None] + NA_COLS)
        rel_col = jnp.clip(kc[None, :] - qc[:, None] + NA_COLS - 1, 0, 2 * NA_COLS - 2)
        k_blk = jnp.take(kg[:, :, kc0:kc0 + NA_KCOLS], row_idx, axis=1)
        v_blk = jnp.take(vg[:, :, kc0:kc0 + NA_KCOLS], row_idx, axis=1)
        s = jnp.einsum("brqhd,brkwhd->brhqkw", qg[:, :, c0:c0 + NA_QCOLS], k_blk,
                       preferred_element_type=jnp.float32)
        bias = rpb[:, rel_row[:, None, :, None], rel_col[None, :, None, :]]
        bias = jnp.transpose(bias, (1, 0, 2, 3, 4)).astype(jnp.float32)
        s = s + bias[None] + jnp.where(col_mask, 0.0, NEG_INF)[:, None, :]
        p = jax.nn.softmax(s.reshape(b, rows, h, NA_QCOLS, kh * NA_KCOLS), axis=-1).reshape(s.shape)
        outs.append(jnp.einsum("brhqkw,brkwhd->brqhd", p.astype(v.dtype), v_blk))
    o = jnp.stack(outs, axis=2)
    return o.reshape(b, t, h * d)


def windowed_gqa(q, k, v, sink):
    b, t, h, d = q.shape
    kvh = k.shape[2]
    g = h // kvh
    nb = t // WIN_BLOCK
    qb = q.reshape(b, nb, WIN_BLOCK, kvh, g, d)
    pad = ((0, 0), (WIN_BLOCK, WIN_BLOCK), (0, 0), (0, 0))
    kp = jnp.pad(k, pad).reshape(b, nb + 2, WIN_BLOCK, kvh, d)
    vp = jnp.pad(v, pad).reshape(b, nb + 2, WIN_BLOCK, kvh, d)
    kb = jnp.concatenate([kp[:, :-2], kp[:, 1:-1], kp[:, 2:]], axis=2)
    vb = jnp.concatenate([vp[:, :-2], vp[:, 1:-1], vp[:, 2:]], axis=2)
    s = jnp.einsum("bnqkgd,bnskd->bnkgqs", qb, kb, preferred_element_type=jnp.float32)
    i = jnp.arange(WIN_BLOCK)
    w = jnp.arange(3 * WIN_BLOCK)
    dist = (i[:, None] - w[None, :] + WIN_BLOCK).astype(jnp.float32)
    pos_s = jnp.arange(nb)[:, None] * WIN_BLOCK - WIN_BLOCK + w[None, :]
    valid = (jnp.abs(dist) <= WINDOW)[None] & ((pos_s >= 0) & (pos_s < t))[:, None, :]
    slopes = alibi_slopes(h).reshape(kvh, g)
    s = s - slopes[:, :, None, None] * jnp.abs(dist)[None, None]
    s = jnp.where(valid[None, :, None, None], s, NEG_INF)
    sink_l = sink.astype(jnp.float32).reshape(kvh, g)[None, None, :, :, None, None]
    m = jnp.maximum(jnp.max(s, axis=-1, keepdims=True), sink_l)
    p = jnp.exp(s - m)
    denom = jnp.sum(p, axis=-1, keepdims=True) + jnp.exp(sink_l - m)
    o = jnp.einsum("bnkgqs,bnskd->bnqkgd", (p / denom).astype(v.dtype), vb)
    return o.reshape(b, t, h * d)


def memory_attention(q, km, vm):
    b, t, h, d = q.shape
    s = jnp.einsum("bthd,bmhd->bhtm", q, km, preferred_element_type=jnp.float32)
    p = jax.nn.softmax(s, axis=-1)
    o = jnp.einsum("bhtm,bmhd->bthd", p.astype(vm.dtype), vm)
    return o.reshape(b, t, h * d)


def layer(x, mem, norm_g, w_in, q_norm_g, k_norm_g, rpb, sink, mem_norm_g, w_mem_kv, w_out):
    b, t, _ = x.shape
    scale = HEAD_DIM ** -0.5
    hn = rms_norm(x, norm_g)
    proj = hn @ w_in
    sizes = [D_NA, D_NA, D_NA, D_NA, D_WIN, D_KV_WIN, D_KV_WIN, D_WIN, D_MEM, D_MEM]
    points = []
    acc = 0
    for sz in sizes[:-1]:
        acc += sz
        points.append(acc)
    na_q, na_k, na_v, na_g, wq, wk, wv, wg, mq, mg = jnp.split(proj, points, axis=-1)
    qa = rms_norm(na_q.reshape(b, t, N_HEADS_NA, HEAD_DIM), q_norm_g[0]) * scale
    ka = rms_norm(na_k.reshape(b, t, N_HEADS_NA, HEAD_DIM), k_norm_g[0])
    va = na_v.reshape(b, t, N_HEADS_NA, HEAD_DIM)
    ya = neighbourhood_attention(qa, ka, va, rpb) * jax.nn.silu(na_g)
    qb = rms_norm(wq.reshape(b, t, N_HEADS_WIN, HEAD_DIM), q_norm_g[1]) * scale
    kb = rms_norm(wk.reshape(b, t, N_KV_WIN, HEAD_DIM), k_norm_g[1])
    vb = wv.reshape(b, t, N_KV_WIN, HEAD_DIM)
    yb = windowed_gqa(qb, kb, vb, sink) * jax.nn.silu(wg)
    n_mem = mem.shape[1]
    mkv = rms_norm(mem, mem_norm_g) @ w_mem_kv
    mk, mv = jnp.split(mkv, 2, axis=-1)
    km = rms_norm(mk.reshape(b, n_mem, N_HEADS_MEM, HEAD_DIM), k_norm_g[2])
    vm = mv.reshape(b, n_mem, N_HEADS_MEM, HEAD_DIM)
    qc = rms_norm(mq.reshape(b, t, N_HEADS_MEM, HEAD_DIM), q_norm_g[2]) * scale
    yc = memory_attention(qc, km, vm) * jax.nn.silu(mg)
    y = jnp.concatenate([ya, yb, yc], axis=-1) @ w_out
    return x + y


def setup_inputs(seed: int = 0) -> dict:
    key = jax.random.key(seed)
    ks = jax.random.split(key, 13)
    f32 = jnp.float32
    x_prompt = jax.random.normal(ks[0], (BATCH, SEQ, D_MODEL), f32)
    x_sample = jax.random.normal(ks[1], (DEC_BATCH, DEC_SEQ, D_MODEL), f32)
    mem_prompt = jax.random.normal(ks[2], (BATCH, N_MEM, D_MODEL), f32)
    mem_sample = jax.random.normal(ks[3], (DEC_BATCH, N_MEM, D_MODEL), f32)
    norm_g = 1.0 + 0.02 * jax.random.normal(ks[4], (DEPTH, D_MODEL), f32)
    w_in = jax.random.normal(ks[5], (DEPTH, D_MODEL, D_IN), f32) * D_MODEL ** -0.5
    q_norm_g = 1.0 + 0.02 * jax.random.normal(ks[6], (DEPTH, 3, HEAD_DIM), f32)
    k_norm_g = 1.0 + 0.02 * jax.random.normal(ks[7], (DEPTH, 3, HEAD_DIM), f32)
    rpb = 0.1 * jax.random.normal(ks[8], (DEPTH, N_HEADS_NA, 2 * NA_ROWS_MAX - 1, 2 * NA_COLS - 1), f32)
    sink = 0.5 * jax.random.normal(ks[9], (DEPTH, N_HEADS_WIN), f32)
    mem_norm_g = 1.0 + 0.02 * jax.random.normal(ks[10], (DEPTH, D_MODEL), f32)
    w_mem_kv = jax.random.normal(ks[11], (DEPTH, D_MODEL, 2 * D_MEM), f32) * D_MODEL ** -0.5
    w_out = jax.random.normal(ks[12], (DEPTH, D_MIX, D_MODEL), f32) * D_MIX ** -0.5
    return {"x_prompt": x_prompt, "x_sample": x_sample, "mem_prompt": mem_prompt, "mem_sample": mem_sample,
            "norm_g": norm_g, "w_in": w_in, "q_norm_g": q_norm_g, "k_norm_g": k_norm_g, "rpb": rpb,
            "sink": sink, "mem_norm_g": mem_norm_g, "w_mem_kv": w_mem_kv, "w_out": w_out}


def reference(x_prompt, x_sample, mem_prompt, mem_sample, norm_g, w_in, q_norm_g, k_norm_g, rpb, sink,
              mem_norm_g, w_mem_kv, w_out):
    y_prompt = x_prompt
    y_sample = x_sample
    for l in range(DEPTH):
        y_prompt = layer(y_prompt, mem_prompt, norm_g[l], w_in[l], q_norm_g[l], k_norm_g[l], rpb[l], sink[l],
                         mem_norm_g[l], w_mem_kv[l], w_out[l])
        y_sample = layer(y_sample, mem_sample, norm_g[l], w_in[l], q_norm_g[l], k_norm_g[l], rpb[l], sink[l],
                         mem_norm_g[l], w_mem_kv[l], w_out[l])
    return (y_prompt, y_sample)
```

```python
import numpy as np
from contextlib import ExitStack
import concourse.bass as bass
import concourse.mybir as mybir
from concourse.bass_utils import run_bass_kernel_spmd

F32 = mybir.dt.float32
BF16 = mybir.dt.bfloat16
AF = mybir.ActivationFunctionType
ALU = mybir.AluOpType

NEG = -30000.0
EPS = 1e-6
LAG = 3
XS = 6
VW = 72
OW = 72
D = 1024
DIN = 3072

C_CM = 0
C_CM_M2 = 128
C_CM_P2 = 256
C_QSCALE = 384
C_MASKAB = 390
NCP = 392
NCA = 256
NCB = 768
SW = 768


class Op:
    __slots__ = ("eng", "fn", "deps", "signal", "val", "sem", "is_dma")


class Prog:
    ENGS = ("pe", "act", "dve", "pool", "sp")

    def __init__(self):
        self.ops = {e: [] for e in self.ENGS}
        self.all = []
        self.last_w = {}
        self.readers = {}

    def add(self, eng, fn, reads=(), writes=(), dma=None):
        op = Op()
        op.eng = eng
        op.fn = fn
        op.is_dma = dma is not None
        op.sem = dma
        op.signal = False
        op.val = 0
        deps = []
        for r in reads:
            w = self.last_w.get(r)
            if w is not None:
                deps.append(w)
        for r in writes:
            w = self.last_w.get(r)
            if w is not None:
                deps.append(w)
            rd = self.readers.get(r)
            if rd:
                deps.extend(rd.values())
        dd = []
        seen = set()
        for d in deps:
            if id(d) in seen or d is op:
                continue
            seen.add(id(d))
            if (not d.is_dma) and d.eng == eng and eng == "pe":
                continue
            dd.append(d)
            d.signal = True
        op.deps = dd
        for r in reads:
            key = dma if dma is not None else eng
            self.readers.setdefault(r, {})[key] = op
        for r in writes:
            self.last_w[r] = op
            self.readers[r] = {}
        self.ops[eng].append(op)
        self.all.append(op)
        return op

    def finalize(self):
        cnt = {}
        for op in self.all:
            if op.is_dma:
                cnt[op.sem] = cnt.get(op.sem, 0) + 16
                op.val = cnt[op.sem]
            elif op.signal:
                cnt[op.eng] = cnt.get(op.eng, 0) + 1
                op.val = cnt[op.eng]
        return sorted(k for k in cnt if k not in self.ENGS)

    def emit_engine(self, eng_name, eng, sems):
        waited = {}
        for op in self.ops[eng_name]:
            need = {}
            for d in op.deps:
                key = d.sem if d.is_dma else d.eng
                if need.get(key, 0) < d.val:
                    need[key] = d.val
            for key, val in need.items():
                if waited.get(key, 0) < val:
                    eng.wait_ge(sems[key], val)
                    waited[key] = val
            if op.fn is None:
                continue
            ins = op.fn(eng)
            if op.is_dma:
                ins.then_inc(sems[op.sem], 16)
            elif op.signal:
                ins.then_inc(sems[op.eng], 1)


def build_program(DEPTH=4, SEG=16, NSEG=3):
    NT = SEG * NSEG
    nc = bass.Bass("TRN2", target_bir_lowering=False)

    def dram(name, shape, kind="ExternalInput"):
        return nc.dram_tensor(name, shape, F32, kind=kind).ap()

    x_in = dram("x", [NT * 128, D])
    y_out = dram("y", [NT * 128, D], kind="ExternalOutput")
    mem_in = dram("mem", [NSEG * 256, D])
    w_in_d = dram("w_in", [DEPTH, D, DIN])
    w_out_d = dram("w_out", [DEPTH, D, D])
    w_mem_d = dram("w_mem", [DEPTH, D, 512])
    gx_d = dram("gx", [DEPTH, 128, 8])
    gm_d = dram("gm", [DEPTH, 128, 8])
    gqk_d = dram("gqk", [DEPTH, 128, 6])
    sink_d = dram("sinkb", [DEPTH, 128, 6])
    rpbT_d = dram("rpbT", [DEPTH * 6, 128, 7 * 128])
    flag_d = dram("flag", [128, 1])
    const_d = dram("consts", [128, NCP])
    constA_d = dram("constsA", [128, NCA])
    constB_d = dram("constsB", [128, NCB])

    P = Prog()
    es = ExitStack()

    def sb(name, shape, dt=F32):
        return es.enter_context(nc.sbuf_tensor(name, shape, dt))

    w_in_sb = sb("w_in_sb", [128, 8, DIN], BF16)
    w_out_sb = sb("w_out_sb", [128, 8, D], BF16)
    w_mem_sb = sb("w_mem_sb", [128, 8, 512], BF16)
    jt = sb("jt", [128, 1])
    xring = [sb(f"xr{i}", [128, D]) for i in range(XS)]
    xs_bfs = [sb(f"xs_bf{i}", [128, D], BF16) for i in range(2)]
    hnTs = [sb(f"hnT{i}", [128, 8, 128], BF16) for i in range(2)]
    sq_bf = [sb(f"sq{i}", [128, 512], BF16) for i in range(2)]
    lr = [sb(f"lr{i}", [128, 512]) for i in range(2)]
    QT = [sb(f"QT{i}", [128, 8, 2, 128], BF16) for i in range(4)]
    KT = [sb(f"KT{i}", [128, 4, 128], BF16) for i in range(8)]
    VR = [sb(f"VR{i}", [128, 8, VW], BF16) for i in range(8)]
    GR = [sb(f"GR{i}", [128, D], BF16) for i in range(4)]
    gt = [sb(f"gt{i}", [128, 512]) for i in range(2)]
    NPT = 4
    PT = [sb(f"PT{i}", [128, 512], BF16) for i in range(NPT)]
    bias_na = sb("bias_na", [128, 6, 9, 128], BF16)
    bias_win = sb("bias_win", [128, 6, 3, 128], BF16)
    bias_winf = sb("bias_winf", [128, 6, 2, 128], BF16)
    rpbst = sb("rpbst", [128, 7, 128])
    consts = sb("consts_sb", [128, NCP])
    ident = sb("ident", [128, 128], BF16)
    onesb = sb("onesb", [128, 128], BF16)
    KmT = [sb(f"KmT{i}", [128, 2, 256], BF16) for i in range(NSEG)]
    Vm = [sb(f"Vm{i}", [128, 2, 4, VW], BF16) for i in range(NSEG)]
    ob_t = sb("ob_t", [128, 6 * OW])
    ob = sb("ob", [128, 6 * OW])
    den = sb("den", [128, 8])
    recip = sb("recip", [128, 8])
    y1 = sb("y1", [128, 6, 64])
    y_bf = sb("y_bf", [128, D], BF16)
    yT = sb("yT", [128, 8, 128], BF16)
    gx_sb = sb("gx_sb", [128, 8])
    gm_sb = sb("gm_sb", [128, 8])
    gqk_sb = sb("gqk_sb", [128, 6])
    gqs = sb("gqs", [128, 6])
    gqm = sb("gqm", [128, 3, 2])
    sink_sb = sb("sink_sb", [128, 6])
    esink = sb("esink", [128, 6])
    flag_sb = sb("flag_sb", [128, 1])
    negf = sb("negf", [128, 1])
    omf = sb("omf", [128, 1])
    ss = sb("ss", [128, 1])
    lnv = sb("lnv", [128, 1])
    rstd = sb("rstd", [128, 1])
    c_eps = sb("c_eps", [128, 1])
    c_one = sb("c_one", [128, 1])

    banks = [es.enter_context(nc.psum_tensor(f"bank{i}", [128, 512], F32)) for i in range(8)]
    B_S = [4, 0, 1, 7]
    B_O = [2, 3]
    B_T = 4
    B_P = [5, 6]
    B_SS = 7
    tp_bf = banks[B_T][:].bitcast(BF16).rearrange("p (c t) -> p c t", c=8)

    def ps(b):
        return ("ps", b)

    P.add("sp", lambda e: e.dma_start(out=consts[:], in_=const_d[:, :]), writes=["consts"], dma="cst")
    P.add("sp", lambda e: e.dma_start(out=flag_sb[:], in_=flag_d[:, :]), writes=["flag"], dma="flg")
    P.add("sp", lambda e: e.dma_start(out=xring[0][:, 0:NCA], in_=constA_d[:, :]), writes=[("xr", 0)], dma="xl0")
    P.add("sp", lambda e: e.dma_start(out=xring[1][:, 0:NCB], in_=constB_d[:, :]), writes=[("xr", 1)], dma="xl1")
    P.add("dve", lambda e: e.memset(c_eps[:], EPS), writes=["c_eps"])
    P.add("dve", lambda e: e.memset(c_one[:], 1.0), writes=["c_one"])
    P.add("dve", lambda e: e.tensor_copy(out=ident[:], in_=xring[0][:, 0:128]),
          reads=[("xr", 0)], writes=["ident"])
    P.add("dve", lambda e: e.tensor_copy(out=onesb[:], in_=xring[0][:, 128:256]),
          reads=[("xr", 0)], writes=["onesb"])
    for i in range(8):
        P.add("dve", lambda e, i=i: e.memset(VR[i][:], 1.0), writes=[("VR", i)])
    for i in range(NSEG):
        P.add("dve", lambda e, i=i: e.memset(Vm[i][:], 1.0), writes=[("Vm", i)])
    P.add("dve", lambda e: e.tensor_scalar(out=negf[:], in0=flag_sb[:], scalar1=-1.0, scalar2=-NEG,
                                           op0=ALU.add, op1=ALU.mult), reads=["flag"], writes=["negf"])
    P.add("dve", lambda e: e.tensor_scalar(out=omf[:], in0=flag_sb[:], scalar1=-1.0, scalar2=1.0,
                                           op0=ALU.mult, op1=ALU.add), reads=["flag"], writes=["omf"])
    absd = xring[1][:, 0:384].rearrange("p (c t) -> p c t", c=3)
    wmask = xring[1][:, 384:768].rearrange("p (c t) -> p c t", c=3)
    for h in range(6):
        slope = float(2.0 ** (-8.0 * (h + 1) / 6.0))
        P.add("dve", lambda e, h=h, slope=slope: e.scalar_tensor_tensor(
            out=bias_win[:, h, :, :], in0=absd, scalar=-slope, in1=wmask, op0=ALU.mult, op1=ALU.add),
            reads=[("xr", 1)], writes=["bias_win"])
    for h in range(6):
        for j, dsel in enumerate((0, 2)):
            P.add("dve", lambda e, h=h, j=j, dsel=dsel: e.tensor_scalar(
                out=bias_winf[:, h, j, :], in0=bias_win[:, h, dsel, :], scalar1=negf[:], scalar2=None,
                op0=ALU.add), reads=["bias_win", "negf"], writes=["bias_winf"])

    state = {"xcnt": 0, "stg": 0, "pbank": 0, "sqi": 0, "gti": 0, "pti": 0, "xsi": 0, "hni": 0, "hcur": 0}
    P_ROT = [5, 6]

    def xalloc():
        s_ = state["xcnt"] % XS
        state["xcnt"] += 1
        return s_

    def load_rows(dram_ap, slot, rd=()):
        P.add("sp", lambda e: e.dma_start(out=xring[slot][:], in_=dram_ap), reads=list(rd),
              writes=[("xr", slot)], dma=f"xl{slot}")

    def next_pbank():
        b_ = P_ROT[state["pbank"] % len(P_ROT)]
        state["pbank"] += 1
        return b_

    xsq = []
    hq = []

    def hn_next():
        state["hcur"] = hq.pop(0)

    def frontA(slot):
        xi = state["xsi"] % 2
        state["xsi"] += 1
        xsq.append(xi)
        xs_bf = xs_bfs[xi]
        xres = ("xs_bf", xi)
        xr = xring[slot]
        P.add("act", lambda e: e.activation(out=xs_bf[:], in_=xr[:], func=AF.Square, accum_out=ss[:]),
              reads=[("xr", slot)], writes=[xres, "ss"])
        P.add("act", lambda e: e.activation(out=lnv[:], in_=ss[:], func=AF.Ln, scale=1.0 / D, bias=c_eps[:]),
              reads=["ss", "c_eps"], writes=["lnv"])
        P.add("act", lambda e: e.activation(out=rstd[:], in_=lnv[:], func=AF.Exp, scale=-0.5),
              reads=["lnv"], writes=["rstd"])
        P.add("act", lambda e: e.activation(out=xs_bf[:], in_=xr[:], func=AF.Copy, scale=rstd[:]),
              reads=[("xr", slot), "rstd"], writes=[xres])

    def frontB(g_sb, g_res, tb=None):
        xi = xsq.pop(0)
        xs_bf = xs_bfs[xi]
        xres = ("xs_bf", xi)
        tb = B_T if tb is None else tb
        tpv = banks[tb][:].bitcast(BF16).rearrange("p (c t) -> p c t", c=8)
        for c in range(8):
            P.add("pe", lambda e, c=c: e.transpose(out=tpv[:, c, :], in_=xs_bf[:, c * 128:(c + 1) * 128],
                                                   identity=ident[:]),
                  reads=[xres, "ident"], writes=[ps(tb)])
        hi = state["hni"] % 2
        state["hni"] += 1
        hq.append(hi)
        hb = hnTs[hi]
        P.add("dve", lambda e: e.tensor_tensor(out=hb[:], in0=tpv,
                                               in1=g_sb[:, :].unsqueeze(2).to_broadcast([128, 8, 128]), op=ALU.mult),
              reads=[g_res], writes=[("hnT", hi), ps(tb)])

    def fm_mm(w_sb, wres, col0, nch, b_=None):
        if b_ is None:
            b_ = next_pbank()
        bk = banks[b_]
        hb = hnTs[state["hcur"]]
        hres = ("hnT", state["hcur"])
        for c in range(nch):
            for k in range(8):
                P.add("pe", lambda e, c=c, k=k: e.matmul(
                    bk[:, c * 128:(c + 1) * 128], lhsT=w_sb[:, k, col0 + c * 128: col0 + (c + 1) * 128],
                    rhs=hb[:, k, :], start=(k == 0), stop=(k == 7)),
                    reads=[hres, wres], writes=[ps(b_)])
        return b_

    def fm_sq(b_, nch):
        si = state["sqi"] % 2
        state["sqi"] += 1
        n = nch * 128
        P.add("act", lambda e: e.activation(out=sq_bf[si][:, 0:n], in_=banks[b_][:, 0:n], func=AF.Square),
              reads=[], writes=[ps(b_), ("sq", si)])
        return si

    def fm_ones(si, nch, sb_=None):
        sb_ = B_SS if sb_ is None else sb_
        n = nch * 128
        P.add("pe", lambda e: e.matmul(banks[sb_][:, 0:n], lhsT=onesb[:], rhs=sq_bf[si][:, 0:n],
                                       start=True, stop=True),
              reads=[("sq", si), "onesb"], writes=[ps(sb_)])

    def fm_lnexp(si, nch, sb_=None):
        sb_ = B_SS if sb_ is None else sb_
        n = nch * 128
        P.add("act", lambda e: e.activation(out=lr[si][:, 0:n], in_=banks[sb_][:, 0:n], func=AF.Ln, bias=c_eps[:]),
              reads=["c_eps"], writes=[ps(sb_), ("lr", si)])
        P.add("act", lambda e: e.activation(out=lr[si][:, 0:n], in_=lr[si][:, 0:n], func=AF.Exp, scale=-0.5),
              reads=[], writes=[("lr", si)])

    def fm_norm(b_, si, runs):
        bk = banks[b_]
        for (c0, cn, g_ap, g_res, out_ap, out_res) in runs:
            P.add("dve", lambda e, c0=c0, cn=cn, g_ap=g_ap, out_ap=out_ap: e.scalar_tensor_tensor(
                out=out_ap, in0=bk[:, c0 * 128:(c0 + cn) * 128].rearrange("p (c t) -> p c t", c=cn),
                scalar=g_ap, in1=lr[si][:, c0 * 128:(c0 + cn) * 128].rearrange("p (c t) -> p c t", c=cn),
                op0=ALU.mult, op1=ALU.mult),
                reads=[("lr", si), g_res], writes=[ps(b_), out_res])

    def load_weights(l):
        def kview(src2d):
            return src2d.rearrange("(k p) c -> p k c", p=128)

        def group(token, sem, pairs):
            ops = []
            for n_, (dst_ap, src_ap) in enumerate(pairs):
                wr = [token] if n_ == 0 else [(token, n_)]
                ops.append(P.add("pool", lambda e, dst_ap=dst_ap, src_ap=src_ap: e.dma_start(out=dst_ap, in_=src_ap),
                                 writes=wr, dma=sem))
            P.add("pool", lambda e: e.memset(jt[:], 0.0),
                  reads=[(token, n_) for n_ in range(1, len(pairs))], writes=[token, "jt"])

        group("w_mem", "wm", [(w_mem_sb[:, :, :], kview(w_mem_d[l, :, :]))])
        wi = w_in_d
        pairs = [
            (w_in_sb[:, :, 0:768], kview(wi[l, :, 0:768])),
            (w_in_sb[:, :, 1536:1920], kview(wi[l, :, 768:1152])),
            (w_in_sb[:, :, 2048:2432], kview(wi[l, :, 1152:1536])),
            (w_in_sb[:, :, 1152:1280], kview(wi[l, :, 1920:2048])),
            (w_in_sb[:, :, 1920:2048], kview(wi[l, :, 2048:2176])),
            (w_in_sb[:, :, 2432:2816], kview(wi[l, :, 2176:2560])),
            (w_in_sb[:, :, 1280:1536], kview(wi[l, :, 2560:2816])),
            (w_in_sb[:, :, 2816:3072], kview(wi[l, :, 2816:3072])),
        ]
        for a_ in range(2):
            for c_ in range(3):
                h_ = a_ * 3 + c_
                d0 = 768 + c_ * 128 + a_ * 64
                pairs.append((w_in_sb[:, :, d0:d0 + 64], kview(wi[l, :, 1536 + h_ * 64:1536 + (h_ + 1) * 64])))
        group("w_in", "wi", pairs)
        group("w_out", "wo", [(w_out_sb[:, :, :], kview(w_out_d[l, :, :]))])

    def layer_params(l):
        P.add("sp", lambda e: e.dma_start(out=gx_sb[:], in_=gx_d[l, :, :]), writes=["gx"], dma="p_gx")
        P.add("sp", lambda e: e.dma_start(out=gm_sb[:], in_=gm_d[l, :, :]), writes=["gm"], dma="p_gm")
        P.add("sp", lambda e: e.dma_start(out=gqk_sb[:], in_=gqk_d[l, :, :]), writes=["gqk"], dma="p_gqk")
        P.add("sp", lambda e: e.dma_start(out=sink_sb[:], in_=sink_d[l, :, :]), writes=["sink"], dma="p_sink")
        P.add("dve", lambda e: e.tensor_tensor(out=gqs[:], in0=gqk_sb[:], in1=consts[:, C_QSCALE:C_QSCALE + 6],
                                               op=ALU.mult), reads=["gqk", "consts"], writes=["gqs"])
        for g_ in range(3):
            P.add("dve", lambda e, g_=g_: e.tensor_scalar(
                out=gqm[:, g_, :], in0=consts[:, C_MASKAB:C_MASKAB + 2], scalar1=gqs[:, 2 * g_:2 * g_ + 1],
                scalar2=None, op0=ALU.mult), reads=["gqs", "consts"], writes=["gqm"])
        P.add("act", lambda e: e.activation(out=esink[:], in_=sink_sb[:], func=AF.Exp),
              reads=["sink"], writes=["esink"])
    def layer_bias(l, heads):
        cm = consts[:, C_CM:C_CM + 128]
        for h in heads:
            P.add("sp", lambda e, h=h: e.dma_start(out=rpbst[:].rearrange("p a b -> p (a b)"), in_=rpbT_d[l * 6 + h, :, :]),
                  writes=["rpbst"], dma="rpb")
            P.add("dve", lambda e, h=h: e.tensor_tensor(
                out=bias_na[:, h, 0:7, :], in0=rpbst[:], in1=cm.unsqueeze(1).to_broadcast([128, 7, 128]), op=ALU.add),
                reads=["rpbst", "consts"], writes=["bias_na"])
            P.add("dve", lambda e, h=h: e.tensor_tensor(
                out=bias_na[:, h, 7, :], in0=rpbst[:, 1, :], in1=consts[:, C_CM_M2:C_CM_M2 + 128], op=ALU.add),
                reads=["rpbst", "consts"], writes=["bias_na"])
            P.add("dve", lambda e, h=h: e.tensor_tensor(
                out=bias_na[:, h, 8, :], in0=rpbst[:, 5, :], in1=consts[:, C_CM_P2:C_CM_P2 + 128], op=ALU.add),
                reads=["rpbst", "consts"], writes=["bias_na"])

    mem_tiles = [(s_, mt) for s_ in range(NSEG) for mt in range(2)]
    mslots = {}

    def mem_load(tis):
        for ti_ in tis:
            slot = xalloc()
            mslots[ti_] = slot
            load_rows(mem_in[ti_ * 128:(ti_ + 1) * 128, :], slot)

    def mem_pre():
        frontA(mslots[0])
        frontA(mslots[1])

    def mem_kv(l):
        tiles = mem_tiles
        frontB(gm_sb, "gm", None)
        if len(tiles) > 2:
            frontA(mslots[2])
        for ti_, (s_, mt) in enumerate(tiles):
            if ti_ + 1 < len(tiles):
                frontB(gm_sb, "gm", 7 if (ti_ + 1) % 2 else None)
                if ti_ + 3 < len(tiles):
                    frontA(mslots[ti_ + 3])
            hn_next()
            b_ = fm_mm(w_mem_sb, "w_mem", 0, 2, 0 if ti_ % 2 else 2)
            si = fm_sq(b_, 2)
            b2 = 1 if ti_ % 2 else 3
            for k in range(8):
                P.add("pe", lambda e, k=k, b2=b2, hb=hnTs[state["hcur"]]: e.matmul(
                    banks[b2][:, 0:256], lhsT=hb[:, k, :], rhs=w_mem_sb[:, k, 256:512], start=(k == 0), stop=(k == 7)),
                    reads=[("hnT", state["hcur"]), "w_mem"], writes=[ps(b2)])
            fm_ones(si, 2)
            fm_lnexp(si, 2)
            fm_norm(b_, si, [(0, 2, gqk_sb[:, 5:6], "gqk", KmT[s_][:, :, mt * 128:(mt + 1) * 128], ("KmT", s_))])
            P.add("dve", lambda e, b2=b2, s_=s_, mt=mt: e.tensor_copy(
                out=Vm[s_][:, mt, :, 0:64], in_=banks[b2][:, 0:256].rearrange("p (h d) -> p h d", h=4)),
                reads=[], writes=[ps(b2), ("Vm", s_)])

    def proj(l, j, hook=None):
        hn_next()
        q = QT[j % 4]
        kk = KT[j % 8]
        qres = ("QT", j % 4)
        kres = ("KT", j % 8)

        def qruns(c0, cn, gi_, dst):
            return [(c0, cn, gqm[:, gi_, sd:sd + 1], "gqm", q[:, dst:dst + cn, sd, :], qres) for sd in range(2)]

        def kruns(c0, cn, col, dst):
            return [(c0, cn, gqk_sb[:, col:col + 1], "gqk", kk[:, dst:dst + cn, :], kres)]

        g_runs = [
            qruns(0, 3, 0, 0) + kruns(3, 1, 1, 0),
            kruns(0, 2, 1, 1) + qruns(2, 2, 1, 3),
            qruns(0, 1, 1, 5) + kruns(1, 1, 3, 3) + qruns(2, 2, 2, 6),
        ]
        b0 = fm_mm(w_in_sb, "w_in", 0, 4, 0)
        s0 = fm_sq(b0, 4)
        b1 = fm_mm(w_in_sb, "w_in", 512, 4, 1)
        s1 = fm_sq(b1, 4)
        if hook is not None:
            hook()
        fm_ones(s0, 4)
        fm_lnexp(s0, 4)
        fm_norm(b0, s0, g_runs[0])
        b2 = fm_mm(w_in_sb, "w_in", 1024, 4, 2)
        fm_ones(s1, 4, 5)
        fm_lnexp(s1, 4, 5)
        fm_norm(b1, s1, g_runs[1])
        s2 = fm_sq(b2, 4)
        bv = 3
        for k in range(8):
            P.add("pe", lambda e, k=k, hb=hnTs[state["hcur"]]: e.matmul(banks[bv][:, :], lhsT=hb[:, k, :],
                                                                        rhs=w_in_sb[:, k, 1536:2048],
                                                                        start=(k == 0), stop=(k == 7)),
                  reads=[("hnT", state["hcur"]), "w_in"], writes=[ps(bv)])
        fm_ones(s2, 4)
        fm_lnexp(s2, 4)
        fm_norm(b2, s2, g_runs[2])
        P.add("dve", lambda e: e.tensor_copy(out=VR[j % 8][:, :, 0:64],
                                             in_=banks[bv][:, :].rearrange("p (h d) -> p h d", h=8)),
              reads=[], writes=[ps(bv), ("VR", j % 8)])
        for gi in range(2):
            bg = 5 + gi
            c0 = 2048 + gi * 512
            for k in range(8):
                P.add("pe", lambda e, k=k, bg=bg, c0=c0, hb=hnTs[state["hcur"]]: e.matmul(
                    banks[bg][:, :], lhsT=hb[:, k, :], rhs=w_in_sb[:, k, c0:c0 + 512],
                    start=(k == 0), stop=(k == 7)),
                    reads=[("hnT", state["hcur"]), "w_in"], writes=[ps(bg)])
            ti = state["gti"] % 2
            state["gti"] += 1
            P.add("act", lambda e, bg=bg, ti=ti: e.activation(out=gt[ti][:], in_=banks[bg][:, :], func=AF.Exp, scale=-1.0),
                  reads=[], writes=[ps(bg), ("gt", ti)])
            P.add("act", lambda e, ti=ti: e.activation(out=gt[ti][:], in_=gt[ti][:], func=AF.Ln, bias=c_one[:]),
                  reads=["c_one"], writes=[("gt", ti)])
            P.add("act", lambda e, ti=ti: e.activation(out=gt[ti][:], in_=gt[ti][:], func=AF.Exp, scale=-1.0),
                  reads=[], writes=[("gt", ti)])
            P.add("dve", lambda e, bg=bg, ti=ti, gi=gi: e.tensor_tensor(
                out=GR[j % 4][:, gi * 512:(gi + 1) * 512], in0=banks[bg][:, :], in1=gt[ti][:], op=ALU.mult),
                reads=[("gt", ti)], writes=[ps(bg), ("GR", j % 4)])

    NA_GEN = [(-2, 7), (-1, 2), (0, 3), (1, 4), (2, 8)]
    NA_S0 = [(0, 3), (1, 4), (2, 5), (3, 6)]
    NA_S1 = [(-1, 2), (0, 3), (1, 4), (2, 5)]
    NA_E1 = [(-2, 1), (-1, 2), (0, 3), (1, 4)]
    NA_E0 = [(-3, 0), (-2, 1), (-1, 2), (0, 3)]

    def na_variant(pos, T):
        if pos == 0:
            return NA_S0
        if pos == 1:
            return NA_S1
        if pos == T - 2:
            return NA_E1
        if pos == T - 1:
            return NA_E0
        return NA_GEN

    def attn(l, i):
        seg = i // SEG
        pos = i % SEG
        q = QT[i % 4]
        qres = ("QT", i % 4)
        special = NSEG >= 2 and (SEG - 2 <= i < SEG + 2)
        groups = []

        def na_blocks(variant):
            blks = []
            for p_ in range(3):
                for (dlt, bid) in variant:
                    kt = i + dlt
                    for sd in range(2):
                        h = 2 * p_ + sd
                        blks.append((h, KT[kt % 8][:, p_, :], q[:, p_, :, :],
                                     bias_na[:, h, bid, :], VR[kt % 8][:, h, 0:65],
                                     [("KT", kt % 8), ("VR", kt % 8), qres, "bias_na"], sd))
            return blks

        if special:
            groups.append(dict(name="naC", bank=B_O[0], H=6, blocks=na_blocks(NA_GEN)))
            groups.append(dict(name="naD", bank=B_O[1], H=6, blocks=na_blocks(na_variant(pos, SEG))))
        else:
            if NSEG >= 2 and i < SEG:
                var = NA_S0 if pos == 0 else (NA_S1 if pos == 1 else NA_GEN)
            elif NSEG >= 2 and i < 2 * SEG:
                var = NA_E1 if pos == SEG - 2 else (NA_E0 if pos == SEG - 1 else NA_GEN)
            else:
                var = na_variant(pos, SEG)
            groups.append(dict(name="na", bank=B_O[0], H=6, blocks=na_blocks(var)))

        wb = []
        for c_ in range(3):
            for dlt in (-1, 0, 1):
                kt = i + dlt
                if kt < 0 or kt >= NT:
                    continue
                kseg = kt // SEG
                use_f = False
                if kseg != seg:
                    if NSEG >= 2 and {kseg, seg} == {0, 1}:
                        use_f = True
                    else:
                        continue
                for sd in range(2):
                    hq = sd * 3 + c_
                    bias_ap = bias_winf[:, hq, 0 if dlt < 0 else 1, :] if use_f else bias_win[:, hq, dlt + 1, :]
                    wb.append((hq, KT[kt % 8][:, 3, :], q[:, 3 + c_, :, :], bias_ap,
                               VR[kt % 8][:, 6 + sd, 0:65],
                               [("KT", kt % 8), ("VR", kt % 8), qres, "bias_win", "bias_winf"], sd))
        groups.append(dict(name="win", bank=None, H=6, blocks=wb))
        mb = []
        for p_ in range(2):
            for mt in range(2):
                for sd in range(2):
                    h = 2 * p_ + sd
                    mb.append((h, KmT[seg][:, p_, mt * 128:(mt + 1) * 128],
                               q[:, 6 + p_, :, :], None, Vm[seg][:, mt, h, 0:65],
                               [("KmT", seg), ("Vm", seg), qres], sd))
        groups.append(dict(name="mem", bank=None, H=4, blocks=mb))

        nb = 0
        for g in groups:
            if g["bank"] is None:
                g["bank"] = B_O[nb % 2]
            nb += 1
        loads = []
        for gi_, g in enumerate(groups):
            blks = [(gi_, bi == len(g["blocks"]) - 1, blk) for bi, blk in enumerate(g["blocks"])]
            for a_ in range(0, len(blks), 4):
                loads.append(blks[a_:a_ + 4])

        def rec_qk(n):
            sbk = B_S[n % len(B_S)]
            ld = loads[n]
            assert len(ld) % 2 == 0
            for j in range(0, len(ld), 2):
                (gi_, lastb, blk) = ld[j]
                h, k_ap, qp_ap, bias_ap, v_ap, rds, sd = blk
                assert sd == 0 and ld[j + 1][2][6] == 1
                out_ap = banks[sbk][:, j * 128:(j + 2) * 128].rearrange("p (s t) -> p s t", s=2)
                P.add("pe", lambda e, out_ap=out_ap, k_ap=k_ap, qp_ap=qp_ap, first=(j == 0): e.matmul(
                    out_ap, lhsT=k_ap, rhs=qp_ap, start=first, stop=False, skip_group_check=True),
                    reads=rds, writes=[ps(sbk)])
            for bpos, (gi_, lastb, blk) in enumerate(ld):
                h, k_ap, qp_ap, bias_ap, v_ap, rds, sd = blk
                out_ap = banks[sbk][:, bpos * 128:(bpos + 1) * 128]
                if bias_ap is not None:
                    P.add("pe", lambda e, out_ap=out_ap, bias_ap=bias_ap: e.matmul(
                        out_ap, lhsT=ident[:], rhs=bias_ap, start=False, stop=False, skip_group_check=True),
                        reads=rds + ["ident"], writes=[ps(sbk)])

        def rec_exp(n):
            sbk = B_S[n % len(B_S)]
            ncols = len(loads[n]) * 128
            pi = state["pti"] % NPT
            state["pti"] += 1
            P.add("act", lambda e: e.activation(out=PT[pi][:, 0:ncols], in_=banks[sbk][:, 0:ncols], func=AF.Exp),
                  reads=[], writes=[ps(sbk), ("PT", pi)])
            return pi

        started = set()

        def rec_pv(n, pi):
            for bpos, (gi_, lastb, blk) in enumerate(loads[n]):
                g = groups[gi_]
                h, k_ap, q_ap, bias_ap, v_ap, rds, sd = blk
                ob_ = g["bank"]
                first = (gi_ not in started)
                started.add(gi_)
                out_ap = banks[ob_][:, h * OW:h * OW + 65]
                P.add("pe", lambda e, out_ap=out_ap, v_ap=v_ap, bpos=bpos, first=first: e.matmul(
                    out_ap, lhsT=PT[pi][:, bpos * 128:(bpos + 1) * 128], rhs=v_ap, start=first, stop=False,
                    skip_group_check=True),
                    reads=rds + [("PT", pi)], writes=[ps(ob_)])
                if lastb:
                    g["done"] = True

        g_slot = i % 4
        cur = {"res": [], "wr": []}

        def normalize(src3, H, off, is_win):
            src_res, src_wr = cur["res"], cur["wr"]
            if is_win:
                P.add("dve", lambda e: e.tensor_tensor(out=den[:, 0:H].unsqueeze(2), in0=src3[:, :, 64:65],
                                                       in1=esink[:, 0:H].unsqueeze(2), op=ALU.add),
                      reads=["esink"] + src_res, writes=["den"] + src_wr)
                P.add("dve", lambda e: e.reciprocal(out=recip[:, 0:H], in_=den[:, 0:H]),
                      reads=["den"], writes=["recip"])
            else:
                P.add("dve", lambda e: e.reciprocal(out=recip[:, 0:H].unsqueeze(2), in_=src3[:, :, 64:65]),
                      reads=src_res, writes=["recip"] + src_wr)
            P.add("dve", lambda e: e.tensor_tensor(out=y1[:, 0:H, :], in0=src3[:, :, 0:64],
                                                   in1=recip[:, 0:H].unsqueeze(2).to_broadcast([128, H, 64]), op=ALU.mult),
                  reads=["recip"] + src_res, writes=["y1"] + src_wr)
            P.add("dve", lambda e: e.tensor_tensor(
                out=y_bf[:, off:off + H * 64].rearrange("p (h d) -> p h d", h=H), in0=y1[:, 0:H, :],
                in1=GR[g_slot][:, off:off + H * 64].rearrange("p (h d) -> p h d", h=H), op=ALU.mult),
                reads=["y1", ("GR", g_slot)], writes=["y_bf"])

        finalized = set()

        def finalize_groups():
            for gi_, g in enumerate(groups):
                if not g.get("done") or gi_ in finalized:
                    continue
                if g["name"] == "naC":
                    continue
                finalized.add(gi_)
                H = g["H"]
                if g["name"] == "naD":
                    gc = groups[0]
                    finalized.add(0)

                    def v3(ap):
                        return ap.rearrange("p (h d) -> p h d", h=6)[:, :, 0:65]
                    oc = v3(banks[gc["bank"]][:, 0:6 * OW])
                    od = v3(banks[g["bank"]][:, 0:6 * OW])
                    P.add("dve", lambda e: e.tensor_scalar(out=v3(ob_t[:]), in0=oc, scalar1=flag_sb[:], scalar2=None,
                                                           op0=ALU.mult),
                          reads=["flag"], writes=["ob_t", ps(gc["bank"])])
                    P.add("dve", lambda e: e.scalar_tensor_tensor(out=v3(ob[:]), in0=od, scalar=omf[:], in1=v3(ob_t[:]),
                                                                  op0=ALU.mult, op1=ALU.add),
                          reads=["omf", "ob_t"], writes=["ob", ps(g["bank"])])
                    cur["res"], cur["wr"] = ["ob"], []
                    normalize(ob[:].rearrange("p (h d) -> p h d", h=6), 6, 0, False)
                else:
                    cur["res"], cur["wr"] = [], [ps(g["bank"])]
                    src3 = banks[g["bank"]][:, 0:H * OW].rearrange("p (h d) -> p h d", h=H)
                    off = {"na": 0, "win": 384, "mem": 768}[g["name"]]
                    normalize(src3, H, off, g["name"] == "win")

        NL = len(loads)
        LA = len(B_S)
        for n in range(min(LA, NL)):
            rec_qk(n)
        for n in range(NL):
            pi = rec_exp(n)
            rec_pv(n, pi)
            finalize_groups()
            if n + LA < NL:
                rec_qk(n + LA)

    def tailT(l, i):
        for c in range(8):
            P.add("pe", lambda e, c=c: e.transpose(out=tp_bf[:, c, :], in_=y_bf[:, c * 128:(c + 1) * 128],
                                                   identity=ident[:]),
                  reads=["y_bf", "ident"], writes=[ps(B_T)])
        P.add("dve", lambda e: e.tensor_copy(out=yT[:], in_=tp_bf), reads=[], writes=["yT", ps(B_T)])

    def tailO(l, i, slot):
        xr = xring[slot]
        for half in range(2):
            bo = half
            for k in range(8):
                P.add("pe", lambda e, k=k, bo=bo, half=half: e.matmul(
                    banks[bo][:, :], lhsT=yT[:, k, :], rhs=w_out_sb[:, k, half * 512:(half + 1) * 512],
                    start=(k == 0), stop=(k == 7)),
                    reads=["yT", "w_out"], writes=[ps(bo)])
            P.add("dve", lambda e, bo=bo, half=half: e.tensor_tensor(
                out=xr[:, half * 512:(half + 1) * 512], in0=banks[bo][:, :], in1=xr[:, half * 512:(half + 1) * 512],
                op=ALU.add), reads=[], writes=[ps(bo), ("xr", slot)])
        st = P.add("sp", lambda e: e.dma_start(out=y_out[i * 128:(i + 1) * 128, :], in_=xr[:]),
                   reads=[("xr", slot)], writes=[("hbm_y", i)], dma=f"xs{slot}")
        return st

    stores = []
    for l in range(DEPTH):
        load_weights(l)
        layer_params(l)
        if l == 0:
            mem_load(range(len(mem_tiles)))
            mem_pre()
        mem_kv(l)
        src = x_in if l == 0 else y_out
        slots = {}

        def issue_load(j, src=src, l=l, slots=slots):
            slot = xalloc()
            slots[j] = slot
            load_rows(src[j * 128:(j + 1) * 128, :], slot, [] if l == 0 else [("hbm_y", j)])

        issue_load(0)
        if NT > 1:
            issue_load(1)
        frontA(slots[0])
        frontB(gx_sb, "gx")
        nxt = l + 1 < DEPTH
        for s in range(NT + 4):
            if s + 1 < NT:
                frontA(slots[s + 1])
            has_tail = 0 <= s - 4 < NT
            hk = (lambda: tailT(l, s - 4)) if has_tail else None
            if s < NT:
                proj(l, s, hook=hk)
            elif has_tail:
                hk()
            if has_tail:
                st = tailO(l, s - 4, slots[s - 4])
                if l == DEPTH - 1:
                    stores.append(st)
            if nxt and s == NT:
                mem_load([0, 1])
            if nxt and s == NT + 1:
                mem_load([2, 3])
            if nxt and s == NT + 3:
                mem_load([4, 5])
                mem_pre()
            if s + 1 < NT:
                frontB(gx_sb, "gx", 7)
            if 0 <= s - 3 < NT:
                attn(l, s - 3)
            if s < 3:
                layer_bias(l, [2 * s, 2 * s + 1])
            if s + 2 < NT:
                issue_load(s + 2)
    fin = Op()
    fin.eng = "sp"
    fin.fn = None
    fin.is_dma = False
    fin.sem = None
    fin.signal = False
    fin.val = 0
    fin.deps = stores[-XS:] if len(stores) >= XS else stores
    P.ops["sp"].append(fin)
    P.all.append(fin)

    dma_keys = P.finalize()
    sems = {}
    for e_ in Prog.ENGS:
        sems[e_] = es.enter_context(nc.semaphore(f"s_{e_}"))
    for k_ in dma_keys:
        sems[k_] = es.enter_context(nc.semaphore(f"d_{k_}"))
    block = es.enter_context(nc.Block())

    @block.tensor
    def _(eng):
        P.emit_engine("pe", eng, sems)

    @block.scalar
    def _(eng):
        P.emit_engine("act", eng, sems)

    @block.vector
    def _(eng):
        P.emit_engine("dve", eng, sems)

    @block.gpsimd
    def _(eng):
        P.emit_engine("pool", eng, sems)

    @block.sync
    def _(eng):
        P.emit_engine("sp", eng, sems)

    es.close()
    return nc


def make_consts():
    k = np.arange(128)
    kr, kc = k // 64, k % 64
    qr, qc = k // 64, k % 64
    cs = np.clip(qc - 8, 0, 48)
    colok = (kc[:, None] >= cs[None, :]) & (kc[:, None] < cs[None, :] + 16)
    m2ok = kr[:, None] >= qr[None, :]
    p2ok = (kr[:, None] == 0) & (qr[None, :] == 1)
    cp = np.zeros((128, NCP), np.float32)
    cp[:, C_CM:C_CM + 128] = np.where(colok, 0.0, NEG)
    cp[:, C_CM_M2:C_CM_M2 + 128] = np.where(colok & m2ok, 0.0, NEG)
    cp[:, C_CM_P2:C_CM_P2 + 128] = np.where(colok & p2ok, 0.0, NEG)
    cp[:, C_QSCALE:C_QSCALE + 6] = np.array([0.125, 1.0, 0.125, 1.0, 0.125, 1.0], np.float32)[None, :]
    cp[:, C_MASKAB] = (k < 64).astype(np.float32)
    cp[:, C_MASKAB + 1] = (k >= 64).astype(np.float32)
    ca = np.zeros((128, NCA), np.float32)
    ca[:, 0:128] = np.eye(128, dtype=np.float32)
    blk = (k[:, None] // 64) == (k[None, :] // 64)
    ca[:, 128:256] = np.where(blk, 1.0 / 64.0, 0.0)
    cb = np.zeros((128, NCB), np.float32)
    for di, dl in enumerate((-1, 0, 1)):
        dist = (k[None, :] - k[:, None] - 128 * dl).astype(np.float32)
        ad = np.abs(dist)
        cb[:, di * 128:(di + 1) * 128] = ad
        cb[:, 384 + di * 128:384 + (di + 1) * 128] = np.where(ad <= 128, 0.0, NEG)
    return cp, ca, cb


def expand_rpb(rpb):
    L = rpb.shape[0]
    k = np.arange(128)
    kr, kc = k // 64, k % 64
    out = np.empty((L, 6, 128, 7, 128), np.float32)
    colidx = np.clip(kc[:, None] - kc[None, :] + 15, 0, 30)
    for di in range(7):
        rowidx = 2 * (di - 3) + kr[:, None] - kr[None, :] + 7
        rowidx = np.clip(rowidx, 0, 14)
        out[:, :, :, di, :] = rpb[:, :, rowidx, colidx]
    return np.ascontiguousarray(out.reshape(L * 6, 128, 7 * 128))


def chunk_cols(g):
    L = g.shape[0]
    return np.ascontiguousarray(g.reshape(L, 8, 128).transpose(0, 2, 1))


_CACHE = {}


def _get_program(depth, seg, nseg):
    key = (depth, seg, nseg)
    if key not in _CACHE:
        _CACHE[key] = build_program(depth, seg, nseg)
    return _CACHE[key]


def run_cores(core_inputs, shared, depth, seg, nseg):
    nc = _get_program(depth, seg, nseg)
    in_maps = []
    for ci in core_inputs:
        m = dict(shared)
        m.update(ci)
        in_maps.append(m)
    res = run_bass_kernel_spmd(nc, in_maps, core_ids=list(range(len(in_maps))))
    return [r["y"] for r in res.results]


def make_shared(norm_g, w_in, q_norm_g, k_norm_g, rpb, sink, mem_norm_g, w_mem_kv, w_out):
    L = norm_g.shape[0]
    gqk = np.empty((L, 128, 6), np.float32)
    for gi in range(3):
        gqk[:, :, 2 * gi] = np.concatenate([q_norm_g[:, gi, :], q_norm_g[:, gi, :]], axis=1)
        gqk[:, :, 2 * gi + 1] = np.concatenate([k_norm_g[:, gi, :], k_norm_g[:, gi, :]], axis=1)
    sinkb = np.ascontiguousarray(np.broadcast_to(sink[:, None, :], (L, 128, 6))).astype(np.float32)
    return {
        "w_in": np.ascontiguousarray(w_in, dtype=np.float32),
        "w_out": np.ascontiguousarray(w_out, dtype=np.float32),
        "w_mem": np.ascontiguousarray(w_mem_kv, dtype=np.float32),
        "gx": chunk_cols(np.asarray(norm_g, np.float32)),
        "gm": chunk_cols(np.asarray(mem_norm_g, np.float32)),
        "gqk": gqk,
        "sinkb": sinkb,
        "rpbT": expand_rpb(np.asarray(rpb, np.float32)),
        "consts": make_consts()[0],
        "constsA": make_consts()[1],
        "constsB": make_consts()[2],
    }


def kernel(x_prompt, x_sample, mem_prompt, mem_sample, norm_g, w_in, q_norm_g, k_norm_g, rpb, sink,
           mem_norm_g, w_mem_kv, w_out):
    x_prompt = np.asarray(x_prompt, np.float32)
    x_sample = np.asarray(x_sample, np.float32)
    mem_prompt = np.asarray(mem_prompt, np.float32)
    mem_sample = np.asarray(mem_sample, np.float32)
    shared = make_shared(np.asarray(norm_g, np.float32), np.asarray(w_in, np.float32), np.asarray(q_norm_g, np.float32),
                         np.asarray(k_norm_g, np.float32), np.asarray(rpb, np.float32), np.asarray(sink, np.float32),
                         np.asarray(mem_norm_g, np.float32), np.asarray(w_mem_kv, np.float32),
                         np.asarray(w_out, np.float32))
    cores = []
    for c in range(8):
        if c < 4:
            xs = np.concatenate([x_prompt[c], x_sample[c]], axis=0)
            mm = np.concatenate([mem_prompt[c], mem_prompt[c], mem_sample[c]], axis=0)
            f = 1.0
        else:
            ids = [4 + 3 * (c - 4) + t for t in range(3)]
            xs = np.concatenate([x_sample[t] for t in ids], axis=0)
            mm = np.concatenate([mem_sample[t] for t in ids], axis=0)
            f = 0.0
        cores.append({"x": np.ascontiguousarray(xs), "mem": np.ascontiguousarray(mm),
                      "flag": np.full((128, 1), f, np.float32)})
    ys = run_cores(cores, shared, 4, 16, 3)
    y_prompt = np.empty_like(x_prompt)
    y_sample = np.empty_like(x_sample)
    for c in range(8):
        y = ys[c]
        if c < 4:
            y_prompt[c] = y[0:4096]
            y_sample[c] = y[4096:6144]
        else:
            for t in range(3):
                y_sample[4 + 3 * (c - 4) + t] = y[t * 2048:(t + 1) * 2048]
    return (y_prompt, y_sample)
```

```python
import numpy as np
from contextlib import ExitStack
import concourse.bass as bass
import concourse.mybir as mybir
from concourse.bass_utils import run_bass_kernel_spmd

F32 = mybir.dt.float32
BF16 = mybir.dt.bfloat16
AF = mybir.ActivationFunctionType
ALU = mybir.AluOpType

NEG = -30000.0
EPS = 1e-6
LAG = 3
XS = 6
VW = 72
OW = 72
D = 1024
DIN = 3072

C_CM = 0
C_CM_M2 = 128
C_CM_P2 = 256
C_QSCALE = 384
C_MASKAB = 390
NCP = 392
NCA = 256
NCB = 768
SW = 768


class Op:
    __slots__ = ("eng", "fn", "deps", "signal", "val", "sem", "is_dma")


class Prog:
    ENGS = ("pe", "act", "dve", "pool", "sp")

    def __init__(self):
        self.ops = {e: [] for e in self.ENGS}
        self.all = []
        self.last_w = {}
        self.readers = {}

    def add(self, eng, fn, reads=(), writes=(), dma=None):
        op = Op()
        op.eng = eng
        op.fn = fn
        op.is_dma = dma is not None
        op.sem = dma
        op.signal = False
        op.val = 0
        deps = []
        for r in reads:
            w = self.last_w.get(r)
            if w is not None:
                deps.append(w)
        for r in writes:
            w = self.last_w.get(r)
            if w is not None:
                deps.append(w)
            rd = self.readers.get(r)
            if rd:
                deps.extend(rd.values())
        dd = []
        seen = set()
        for d in deps:
            if id(d) in seen or d is op:
                continue
            seen.add(id(d))
            if (not d.is_dma) and d.eng == eng and eng == "pe":
                continue
            dd.append(d)
            d.signal = True
        op.deps = dd
        for r in reads:
            key = dma if dma is not None else eng
            self.readers.setdefault(r, {})[key] = op
        for r in writes:
            self.last_w[r] = op
            self.readers[r] = {}
        self.ops[eng].append(op)
        self.all.append(op)
        return op

    def finalize(self):
        cnt = {}
        for op in self.all:
            if op.is_dma:
                cnt[op.sem] = cnt.get(op.sem, 0) + 16
                op.val = cnt[op.sem]
            elif op.signal:
                cnt[op.eng] = cnt.get(op.eng, 0) + 1
                op.val = cnt[op.eng]
        return sorted(k for k in cnt if k not in self.ENGS)

    def emit_engine(self, eng_name, eng, sems):
        waited = {}
        for op in self.ops[eng_name]:
            need = {}
            for d in op.deps:
                key = d.sem if d.is_dma else d.eng
                if need.get(key, 0) < d.val:
                    need[key] = d.val
            for key, val in need.items():
                if waited.get(key, 0) < val:
                    eng.wait_ge(sems[key], val)
                    waited[key] = val
            if op.fn is None:
                continue
            ins = op.fn(eng)
            if op.is_dma:
                ins.then_inc(sems[op.sem], 16)
            elif op.signal:
                ins.then_inc(sems[op.eng], 1)


def build_program(DEPTH=4, SEG=16, NSEG=3):
    NT = SEG * NSEG
    nc = bass.Bass("TRN2", target_bir_lowering=False)

    def dram(name, shape, kind="ExternalInput"):
        return nc.dram_tensor(name, shape, F32, kind=kind).ap()

    x_in = dram("x", [NT * 128, D])
    y_out = dram("y", [NT * 128, D], kind="ExternalOutput")
    mem_in = dram("mem", [NSEG * 256, D])
    w_in_d = dram("w_in", [DEPTH, D, DIN])
    w_out_d = dram("w_out", [DEPTH, D, D])
    w_mem_d = dram("w_mem", [DEPTH, D, 512])
    gx_d = dram("gx", [DEPTH, 128, 8])
    gm_d = dram("gm", [DEPTH, 128, 8])
    gqk_d = dram("gqk", [DEPTH, 128, 6])
    sink_d = dram("sinkb", [DEPTH, 128, 6])
    rpbT_d = dram("rpbT", [DEPTH * 6, 128, 7 * 128])
    flag_d = dram("flag", [128, 1])
    const_d = dram("consts", [128, NCP])
    constA_d = dram("constsA", [128, NCA])
    constB_d = dram("constsB", [128, NCB])

    P = Prog()
    es = ExitStack()

    def sb(name, shape, dt=F32):
        return es.enter_context(nc.sbuf_tensor(name, shape, dt))

    w_in_sb = sb("w_in_sb", [128, 8, DIN], BF16)
    w_out_sb = sb("w_out_sb", [128, 8, D], BF16)
    w_mem_sb = sb("w_mem_sb", [128, 8, 512], BF16)
    jt = sb("jt", [128, 1])
    xring = [sb(f"xr{i}", [128, D]) for i in range(XS)]
    xs_bfs = [sb(f"xs_bf{i}", [128, D], BF16) for i in range(2)]
    hnTs = [sb(f"hnT{i}", [128, 8, 128], BF16) for i in range(2)]
    sq_bf = [sb(f"sq{i}", [128, 512], BF16) for i in range(2)]
    lr = [sb(f"lr{i}", [128, 512]) for i in range(2)]
    QT = [sb(f"QT{i}", [128, 8, 2, 128], BF16) for i in range(4)]
    KT = [sb(f"KT{i}", [128, 4, 128], BF16) for i in range(8)]
    VR = [sb(f"VR{i}", [128, 8, VW], BF16) for i in range(8)]
    GR = [sb(f"GR{i}", [128, D], BF16) for i in range(4)]
    gt = [sb(f"gt{i}", [128, 512]) for i in range(2)]
    NPT = 4
    PT = [sb(f"PT{i}", [128, 512], BF16) for i in range(NPT)]
    bias_na = sb("bias_na", [128, 3, 9, 2, 128], BF16)
    bias_win = sb("bias_win", [128, 3, 3, 2, 128], BF16)
    bias_winf = sb("bias_winf", [128, 3, 2, 2, 128], BF16)
    rpbst = sb("rpbst", [128, 7, 128])
    consts = sb("consts_sb", [128, NCP])
    ident = sb("ident", [128, 128], BF16)
    onesb = sb("onesb", [128, 128], BF16)
    KmT = [sb(f"KmT{i}", [128, 2, 256], BF16) for i in range(NSEG)]
    Vm = [sb(f"Vm{i}", [128, 2, 4, VW], BF16) for i in range(NSEG)]
    ob_t = sb("ob_t", [128, 6 * OW])
    ob = sb("ob", [128, 6 * OW])
    den = sb("den", [128, 8])
    recip = sb("recip", [128, 8])
    y1 = sb("y1", [128, 6, 64])
    y_bf = sb("y_bf", [128, D], BF16)
    yT = sb("yT", [128, 8, 128], BF16)
    gx_sb = sb("gx_sb", [128, 8])
    gm_sb = sb("gm_sb", [128, 8])
    gqk_sb = sb("gqk_sb", [128, 6])
    gqs = sb("gqs", [128, 6])
    gqm = sb("gqm", [128, 3, 2])
    sink_sb = sb("sink_sb", [128, 6])
    esink = sb("esink", [128, 6])
    flag_sb = sb("flag_sb", [128, 1])
    negf = sb("negf", [128, 1])
    omf = sb("omf", [128, 1])
    ss = sb("ss", [128, 1])
    lnv = sb("lnv", [128, 1])
    rstd = sb("rstd", [128, 1])
    c_eps = sb("c_eps", [128, 1])
    c_one = sb("c_one", [128, 1])

    banks = [es.enter_context(nc.psum_tensor(f"bank{i}", [128, 512], F32)) for i in range(8)]
    B_S = [4, 0, 1, 7]
    B_O = [2, 3]
    B_T = 4
    B_P = [5, 6]
    B_SS = 7
    tp_bf = banks[B_T][:].bitcast(BF16).rearrange("p (c t) -> p c t", c=8)

    def ps(b):
        return ("ps", b)

    P.add("sp", lambda e: e.dma_start(out=consts[:], in_=const_d[:, :]), writes=["consts"], dma="cst")
    P.add("sp", lambda e: e.dma_start(out=flag_sb[:], in_=flag_d[:, :]), writes=["flag"], dma="flg")
    P.add("sp", lambda e: e.dma_start(out=xring[0][:, 0:NCA], in_=constA_d[:, :]), writes=[("xr", 0)], dma="xl0")
    P.add("sp", lambda e: e.dma_start(out=xring[1][:, 0:NCB], in_=constB_d[:, :]), writes=[("xr", 1)], dma="xl1")
    P.add("dve", lambda e: e.memset(c_eps[:], EPS), writes=["c_eps"])
    P.add("dve", lambda e: e.memset(c_one[:], 1.0), writes=["c_one"])
    P.add("dve", lambda e: e.tensor_copy(out=ident[:], in_=xring[0][:, 0:128]),
          reads=[("xr", 0)], writes=["ident"])
    P.add("dve", lambda e: e.tensor_copy(out=onesb[:], in_=xring[0][:, 128:256]),
          reads=[("xr", 0)], writes=["onesb"])
    for i in range(8):
        P.add("dve", lambda e, i=i: e.memset(VR[i][:], 1.0), writes=[("VR", i)])
    for i in range(NSEG):
        P.add("dve", lambda e, i=i: e.memset(Vm[i][:], 1.0), writes=[("Vm", i)])
    P.add("dve", lambda e: e.tensor_scalar(out=negf[:], in0=flag_sb[:], scalar1=-1.0, scalar2=-NEG,
                                           op0=ALU.add, op1=ALU.mult), reads=["flag"], writes=["negf"])
    P.add("dve", lambda e: e.tensor_scalar(out=omf[:], in0=flag_sb[:], scalar1=-1.0, scalar2=1.0,
                                           op0=ALU.mult, op1=ALU.add), reads=["flag"], writes=["omf"])
    absd = xring[1][:, 0:384].rearrange("p (c t) -> p c t", c=3)
    wmask = xring[1][:, 384:768].rearrange("p (c t) -> p c t", c=3)
    for h in range(6):
        slope = float(2.0 ** (-8.0 * (h + 1) / 6.0))
        P.add("dve", lambda e, h=h, slope=slope: e.scalar_tensor_tensor(
            out=bias_win[:, h % 3, :, h // 3, :], in0=absd, scalar=-slope, in1=wmask, op0=ALU.mult, op1=ALU.add),
            reads=[("xr", 1)], writes=["bias_win"])
    for h in range(6):
        for j, dsel in enumerate((0, 2)):
            P.add("dve", lambda e, h=h, j=j, dsel=dsel: e.tensor_scalar(
                out=bias_winf[:, h % 3, j, h // 3, :], in0=bias_win[:, h % 3, dsel, h // 3, :], scalar1=negf[:], scalar2=None,
                op0=ALU.add), reads=["bias_win", "negf"], writes=["bias_winf"])

    state = {"xcnt": 0, "stg": 0, "pbank": 0, "sqi": 0, "gti": 0, "pti": 0, "xsi": 0, "hni": 0, "hcur": 0}
    P_ROT = [5, 6]

    def xalloc():
        s_ = state["xcnt"] % XS
        state["xcnt"] += 1
        return s_

    def load_rows(dram_ap, slot, rd=()):
        P.add("sp", lambda e: e.dma_start(out=xring[slot][:], in_=dram_ap), reads=list(rd),
              writes=[("xr", slot)], dma=f"xl{slot}")

    def next_pbank():
        b_ = P_ROT[state["pbank"] % len(P_ROT)]
        state["pbank"] += 1
        return b_

    xsq = []
    hq = []

    def hn_next():
        state["hcur"] = hq.pop(0)

    def frontA(slot):
        xi = state["xsi"] % 2
        state["xsi"] += 1
        xsq.append(xi)
        xs_bf = xs_bfs[xi]
        xres = ("xs_bf", xi)
        xr = xring[slot]
        P.add("act", lambda e: e.activation(out=xs_bf[:], in_=xr[:], func=AF.Square, accum_out=ss[:]),
              reads=[("xr", slot)], writes=[xres, "ss"])
        P.add("act", lambda e: e.activation(out=lnv[:], in_=ss[:], func=AF.Ln, scale=1.0 / D, bias=c_eps[:]),
              reads=["ss", "c_eps"], writes=["lnv"])
        P.add("act", lambda e: e.activation(out=rstd[:], in_=lnv[:], func=AF.Exp, scale=-0.5),
              reads=["lnv"], writes=["rstd"])
        P.add("act", lambda e: e.activation(out=xs_bf[:], in_=xr[:], func=AF.Copy, scale=rstd[:]),
              reads=[("xr", slot), "rstd"], writes=[xres])

    def frontB(g_sb, g_res, tb=None):
        xi = xsq.pop(0)
        xs_bf = xs_bfs[xi]
        xres = ("xs_bf", xi)
        tb = B_T if tb is None else tb
        tpv = banks[tb][:].bitcast(BF16).rearrange("p (c t) -> p c t", c=8)
        for c in range(8):
            P.add("pe", lambda e, c=c: e.transpose(out=tpv[:, c, :], in_=xs_bf[:, c * 128:(c + 1) * 128],
                                                   identity=ident[:]),
                  reads=[xres, "ident"], writes=[ps(tb)])
        hi = state["hni"] % 2
        state["hni"] += 1
        hq.append(hi)
        hb = hnTs[hi]
        P.add("dve", lambda e: e.tensor_tensor(out=hb[:], in0=tpv,
                                               in1=g_sb[:, :].unsqueeze(2).to_broadcast([128, 8, 128]), op=ALU.mult),
              reads=[g_res], writes=[("hnT", hi), ps(tb)])

    def fm_mm(w_sb, wres, col0, nch, b_=None):
        if b_ is None:
            b_ = next_pbank()
        bk = banks[b_]
        hb = hnTs[state["hcur"]]
        hres = ("hnT", state["hcur"])
        for c in range(nch):
            for k in range(8):
                P.add("pe", lambda e, c=c, k=k: e.matmul(
                    bk[:, c * 128:(c + 1) * 128], lhsT=w_sb[:, k, col0 + c * 128: col0 + (c + 1) * 128],
                    rhs=hb[:, k, :], start=(k == 0), stop=(k == 7)),
                    reads=[hres, wres], writes=[ps(b_)])
        return b_

    def fm_sq(b_, nch):
        si = state["sqi"] % 2
        state["sqi"] += 1
        n = nch * 128
        P.add("act", lambda e: e.activation(out=sq_bf[si][:, 0:n], in_=banks[b_][:, 0:n], func=AF.Square),
              reads=[], writes=[ps(b_), ("sq", si)])
        return si

    def fm_ones(si, nch, sb_=None):
        sb_ = B_SS if sb_ is None else sb_
        n = nch * 128
        P.add("pe", lambda e: e.matmul(banks[sb_][:, 0:n], lhsT=onesb[:], rhs=sq_bf[si][:, 0:n],
                                       start=True, stop=True),
              reads=[("sq", si), "onesb"], writes=[ps(sb_)])

    def fm_lnexp(si, nch, sb_=None):
        sb_ = B_SS if sb_ is None else sb_
        n = nch * 128
        P.add("act", lambda e: e.activation(out=lr[si][:, 0:n], in_=banks[sb_][:, 0:n], func=AF.Ln, bias=c_eps[:]),
              reads=["c_eps"], writes=[ps(sb_), ("lr", si)])
        P.add("act", lambda e: e.activation(out=lr[si][:, 0:n], in_=lr[si][:, 0:n], func=AF.Exp, scale=-0.5),
              reads=[], writes=[("lr", si)])

    def fm_norm(b_, si, runs):
        bk = banks[b_]
        for (c0, cn, g_ap, g_res, out_ap, out_res) in runs:
            P.add("dve", lambda e, c0=c0, cn=cn, g_ap=g_ap, out_ap=out_ap: e.scalar_tensor_tensor(
                out=out_ap, in0=bk[:, c0 * 128:(c0 + cn) * 128].rearrange("p (c t) -> p c t", c=cn),
                scalar=g_ap, in1=lr[si][:, c0 * 128:(c0 + cn) * 128].rearrange("p (c t) -> p c t", c=cn),
                op0=ALU.mult, op1=ALU.mult),
                reads=[("lr", si), g_res], writes=[ps(b_), out_res])

    def load_weights(l):
        def kview(src2d):
            return src2d.rearrange("(k p) c -> p k c", p=128)

        def group(token, sem, pairs):
            ops = []
            for n_, (dst_ap, src_ap) in enumerate(pairs):
                wr = [token] if n_ == 0 else [(token, n_)]
                ops.append(P.add("pool", lambda e, dst_ap=dst_ap, src_ap=src_ap: e.dma_start(out=dst_ap, in_=src_ap),
                                 writes=wr, dma=sem))
            P.add("pool", lambda e: e.memset(jt[:], 0.0),
                  reads=[(token, n_) for n_ in range(1, len(pairs))], writes=[token, "jt"])

        group("w_mem", "wm", [(w_mem_sb[:, :, :], kview(w_mem_d[l, :, :]))])
        wi = w_in_d
        pairs = [
            (w_in_sb[:, :, 0:768], kview(wi[l, :, 0:768])),
            (w_in_sb[:, :, 1536:1920], kview(wi[l, :, 768:1152])),
            (w_in_sb[:, :, 2048:2432], kview(wi[l, :, 1152:1536])),
            (w_in_sb[:, :, 1152:1280], kview(wi[l, :, 1920:2048])),
            (w_in_sb[:, :, 1920:2048], kview(wi[l, :, 2048:2176])),
            (w_in_sb[:, :, 2432:2816], kview(wi[l, :, 2176:2560])),
            (w_in_sb[:, :, 1280:1536], kview(wi[l, :, 2560:2816])),
            (w_in_sb[:, :, 2816:3072], kview(wi[l, :, 2816:3072])),
        ]
        for a_ in range(2):
            for c_ in range(3):
                h_ = a_ * 3 + c_
                d0 = 768 + c_ * 128 + a_ * 64
                pairs.append((w_in_sb[:, :, d0:d0 + 64], kview(wi[l, :, 1536 + h_ * 64:1536 + (h_ + 1) * 64])))
        group("w_in", "wi", pairs)
        group("w_out", "wo", [(w_out_sb[:, :, :], kview(w_out_d[l, :, :]))])

    def layer_params(l):
        P.add("sp", lambda e: e.dma_start(out=gx_sb[:], in_=gx_d[l, :, :]), writes=["gx"], dma="p_gx")
        P.add("sp", lambda e: e.dma_start(out=gm_sb[:], in_=gm_d[l, :, :]), writes=["gm"], dma="p_gm")
        P.add("sp", lambda e: e.dma_start(out=gqk_sb[:], in_=gqk_d[l, :, :]), writes=["gqk"], dma="p_gqk")
        P.add("sp", lambda e: e.dma_start(out=sink_sb[:], in_=sink_d[l, :, :]), writes=["sink"], dma="p_sink")
        P.add("dve", lambda e: e.tensor_tensor(out=gqs[:], in0=gqk_sb[:], in1=consts[:, C_QSCALE:C_QSCALE + 6],
                                               op=ALU.mult), reads=["gqk", "consts"], writes=["gqs"])
        for g_ in range(3):
            P.add("dve", lambda e, g_=g_: e.tensor_scalar(
                out=gqm[:, g_, :], in0=consts[:, C_MASKAB:C_MASKAB + 2], scalar1=gqs[:, 2 * g_:2 * g_ + 1],
                scalar2=None, op0=ALU.mult), reads=["gqs", "consts"], writes=["gqm"])
        P.add("act", lambda e: e.activation(out=esink[:], in_=sink_sb[:], func=AF.Exp),
              reads=["sink"], writes=["esink"])
    def layer_bias(l, heads):
        cm = consts[:, C_CM:C_CM + 128]
        for h in heads:
            P.add("sp", lambda e, h=h: e.dma_start(out=rpbst[:].rearrange("p a b -> p (a b)"), in_=rpbT_d[l * 6 + h, :, :]),
                  writes=["rpbst"], dma="rpb")
            P.add("dve", lambda e, h=h: e.tensor_tensor(
                out=bias_na[:, h // 2, 0:7, h % 2, :], in0=rpbst[:], in1=cm.unsqueeze(1).to_broadcast([128, 7, 128]), op=ALU.add),
                reads=["rpbst", "consts"], writes=["bias_na"])
            P.add("dve", lambda e, h=h: e.tensor_tensor(
                out=bias_na[:, h // 2, 7, h % 2, :], in0=rpbst[:, 1, :], in1=consts[:, C_CM_M2:C_CM_M2 + 128], op=ALU.add),
                reads=["rpbst", "consts"], writes=["bias_na"])
            P.add("dve", lambda e, h=h: e.tensor_tensor(
                out=bias_na[:, h // 2, 8, h % 2, :], in0=rpbst[:, 5, :], in1=consts[:, C_CM_P2:C_CM_P2 + 128], op=ALU.add),
                reads=["rpbst", "consts"], writes=["bias_na"])

    mem_tiles = [(s_, mt) for s_ in range(NSEG) for mt in range(2)]
    mslots = {}

    def mem_load(tis):
        for ti_ in tis:
            slot = xalloc()
            mslots[ti_] = slot
            load_rows(mem_in[ti_ * 128:(ti_ + 1) * 128, :], slot)

    def mem_pre():
        frontA(mslots[0])
        frontA(mslots[1])

    def mem_kv(l):
        tiles = mem_tiles
        frontB(gm_sb, "gm", None)
        if len(tiles) > 2:
            frontA(mslots[2])
        for ti_, (s_, mt) in enumerate(tiles):
            if ti_ + 1 < len(tiles):
                frontB(gm_sb, "gm", 7 if (ti_ + 1) % 2 else None)
                if ti_ + 3 < len(tiles):
                    frontA(mslots[ti_ + 3])
            hn_next()
            b_ = fm_mm(w_mem_sb, "w_mem", 0, 2, 0 if ti_ % 2 else 2)
            si = fm_sq(b_, 2)
            b2 = 1 if ti_ % 2 else 3
            for k in range(8):
                P.add("pe", lambda e, k=k, b2=b2, hb=hnTs[state["hcur"]]: e.matmul(
                    banks[b2][:, 0:256], lhsT=hb[:, k, :], rhs=w_mem_sb[:, k, 256:512], start=(k == 0), stop=(k == 7)),
                    reads=[("hnT", state["hcur"]), "w_mem"], writes=[ps(b2)])
            fm_ones(si, 2)
            fm_lnexp(si, 2)
            fm_norm(b_, si, [(0, 2, gqk_sb[:, 5:6], "gqk", KmT[s_][:, :, mt * 128:(mt + 1) * 128], ("KmT", s_))])
            P.add("dve", lambda e, b2=b2, s_=s_, mt=mt: e.tensor_copy(
                out=Vm[s_][:, mt, :, 0:64], in_=banks[b2][:, 0:256].rearrange("p (h d) -> p h d", h=4)),
                reads=[], writes=[ps(b2), ("Vm", s_)])

    def proj(l, j, hook=None):
        hn_next()
        q = QT[j % 4]
        kk = KT[j % 8]
        qres = ("QT", j % 4)
        kres = ("KT", j % 8)

        def qruns(c0, cn, gi_, dst):
            return [(c0, cn, gqm[:, gi_, sd:sd + 1], "gqm", q[:, dst:dst + cn, sd, :], qres) for sd in range(2)]

        def kruns(c0, cn, col, dst):
            return [(c0, cn, gqk_sb[:, col:col + 1], "gqk", kk[:, dst:dst + cn, :], kres)]

        g_runs = [
            qruns(0, 3, 0, 0) + kruns(3, 1, 1, 0),
            kruns(0, 2, 1, 1) + qruns(2, 2, 1, 3),
            qruns(0, 1, 1, 5) + kruns(1, 1, 3, 3) + qruns(2, 2, 2, 6),
        ]
        b0 = fm_mm(w_in_sb, "w_in", 0, 4, 0)
        s0 = fm_sq(b0, 4)
        b1 = fm_mm(w_in_sb, "w_in", 512, 4, 1)
        s1 = fm_sq(b1, 4)
        if hook is not None:
            hook()
        fm_ones(s0, 4)
        fm_lnexp(s0, 4)
        fm_norm(b0, s0, g_runs[0])
        b2 = fm_mm(w_in_sb, "w_in", 1024, 4, 2)
        fm_ones(s1, 4, 5)
        fm_lnexp(s1, 4, 5)
        fm_norm(b1, s1, g_runs[1])
        s2 = fm_sq(b2, 4)
        bv = 3
        for k in range(8):
            P.add("pe", lambda e, k=k, hb=hnTs[state["hcur"]]: e.matmul(banks[bv][:, :], lhsT=hb[:, k, :],
                                                                        rhs=w_in_sb[:, k, 1536:2048],
                                                                        start=(k == 0), stop=(k == 7)),
                  reads=[("hnT", state["hcur"]), "w_in"], writes=[ps(bv)])
        fm_ones(s2, 4)
        fm_lnexp(s2, 4)
        fm_norm(b2, s2, g_runs[2])
        P.add("dve", lambda e: e.tensor_copy(out=VR[j % 8][:, :, 0:64],
                                             in_=banks[bv][:, :].rearrange("p (h d) -> p h d", h=8)),
              reads=[], writes=[ps(bv), ("VR", j % 8)])
        for gi in range(2):
            bg = 5 + gi
            c0 = 2048 + gi * 512
            for k in range(8):
                P.add("pe", lambda e, k=k, bg=bg, c0=c0, hb=hnTs[state["hcur"]]: e.matmul(
                    banks[bg][:, :], lhsT=hb[:, k, :], rhs=w_in_sb[:, k, c0:c0 + 512],
                    start=(k == 0), stop=(k == 7)),
                    reads=[("hnT", state["hcur"]), "w_in"], writes=[ps(bg)])
            ti = state["gti"] % 2
            state["gti"] += 1
            P.add("act", lambda e, bg=bg, ti=ti: e.activation(out=gt[ti][:], in_=banks[bg][:, :], func=AF.Exp, scale=-1.0),
                  reads=[], writes=[ps(bg), ("gt", ti)])
            P.add("act", lambda e, ti=ti: e.activation(out=gt[ti][:], in_=gt[ti][:], func=AF.Ln, bias=c_one[:]),
                  reads=["c_one"], writes=[("gt", ti)])
            P.add("act", lambda e, ti=ti: e.activation(out=gt[ti][:], in_=gt[ti][:], func=AF.Exp, scale=-1.0),
                  reads=[], writes=[("gt", ti)])
            P.add("dve", lambda e, bg=bg, ti=ti, gi=gi: e.tensor_tensor(
                out=GR[j % 4][:, gi * 512:(gi + 1) * 512], in0=banks[bg][:, :], in1=gt[ti][:], op=ALU.mult),
                reads=[("gt", ti)], writes=[ps(bg), ("GR", j % 4)])

    NA_GEN = [(-2, 7), (-1, 2), (0, 3), (1, 4), (2, 8)]
    NA_S0 = [(0, 3), (1, 4), (2, 5), (3, 6)]
    NA_S1 = [(-1, 2), (0, 3), (1, 4), (2, 5)]
    NA_E1 = [(-2, 1), (-1, 2), (0, 3), (1, 4)]
    NA_E0 = [(-3, 0), (-2, 1), (-1, 2), (0, 3)]

    def na_variant(pos, T):
        if pos == 0:
            return NA_S0
        if pos == 1:
            return NA_S1
        if pos == T - 2:
            return NA_E1
        if pos == T - 1:
            return NA_E0
        return NA_GEN

    def attn(l, i):
        seg = i // SEG
        pos = i % SEG
        q = QT[i % 4]
        qres = ("QT", i % 4)
        special = NSEG >= 2 and (SEG - 2 <= i < SEG + 2)
        groups = []

        def na_blocks(variant):
            blks = []
            for p_ in range(3):
                for (dlt, bid) in variant:
                    kt = i + dlt
                    for sd in range(2):
                        h = 2 * p_ + sd
                        blks.append((h, KT[kt % 8][:, p_, :], q[:, p_, :, :],
                                     bias_na[:, p_, bid, :, :], VR[kt % 8][:, h, 0:65],
                                     [("KT", kt % 8), ("VR", kt % 8), qres, "bias_na"], sd))
            return blks

        if special:
            groups.append(dict(name="naC", bank=B_O[0], H=6, blocks=na_blocks(NA_GEN)))
            groups.append(dict(name="naD", bank=B_O[1], H=6, blocks=na_blocks(na_variant(pos, SEG))))
        else:
            if NSEG >= 2 and i < SEG:
                var = NA_S0 if pos == 0 else (NA_S1 if pos == 1 else NA_GEN)
            elif NSEG >= 2 and i < 2 * SEG:
                var = NA_E1 if pos == SEG - 2 else (NA_E0 if pos == SEG - 1 else NA_GEN)
            else:
                var = na_variant(pos, SEG)
            groups.append(dict(name="na", bank=B_O[0], H=6, blocks=na_blocks(var)))

        wb = []
        for c_ in range(3):
            for dlt in (-1, 0, 1):
                kt = i + dlt
                if kt < 0 or kt >= NT:
                    continue
                kseg = kt // SEG
                use_f = False
                if kseg != seg:
                    if NSEG >= 2 and {kseg, seg} == {0, 1}:
                        use_f = True
                    else:
                        continue
                for sd in range(2):
                    hq = sd * 3 + c_
                    bias_ap = bias_winf[:, c_, 0 if dlt < 0 else 1, :, :] if use_f else bias_win[:, c_, dlt + 1, :, :]
                    wb.append((hq, KT[kt % 8][:, 3, :], q[:, 3 + c_, :, :], bias_ap,
                               VR[kt % 8][:, 6 + sd, 0:65],
                               [("KT", kt % 8), ("VR", kt % 8), qres, "bias_win", "bias_winf"], sd))
        groups.append(dict(name="win", bank=None, H=6, blocks=wb))
        mb = []
        for p_ in range(2):
            for mt in range(2):
                for sd in range(2):
                    h = 2 * p_ + sd
                    mb.append((h, KmT[seg][:, p_, mt * 128:(mt + 1) * 128],
                               q[:, 6 + p_, :, :], None, Vm[seg][:, mt, h, 0:65],
                               [("KmT", seg), ("Vm", seg), qres], sd))
        groups.append(dict(name="mem", bank=None, H=4, blocks=mb))

        nb = 0
        for g in groups:
            if g["bank"] is None:
                g["bank"] = B_O[nb % 2]
            nb += 1
        loads = []
        for gi_, g in enumerate(groups):
            blks = [(gi_, bi == len(g["blocks"]) - 1, blk) for bi, blk in enumerate(g["blocks"])]
            for a_ in range(0, len(blks), 4):
                loads.append(blks[a_:a_ + 4])

        def rec_qk(n):
            sbk = B_S[n % len(B_S)]
            ld = loads[n]
            assert len(ld) % 2 == 0
            for j in range(0, len(ld), 2):
                (gi_, lastb, blk) = ld[j]
                h, k_ap, qp_ap, bias_ap, v_ap, rds, sd = blk
                assert sd == 0 and ld[j + 1][2][6] == 1
                out_ap = banks[sbk][:, j * 128:(j + 2) * 128].rearrange("p (s t) -> p s t", s=2)
                P.add("pe", lambda e, out_ap=out_ap, k_ap=k_ap, qp_ap=qp_ap, first=(j == 0): e.matmul(
                    out_ap, lhsT=k_ap, rhs=qp_ap, start=first, stop=False, skip_group_check=True),
                    reads=rds, writes=[ps(sbk)])
            for j in range(0, len(ld), 2):
                (gi_, lastb, blk) = ld[j]
                h, k_ap, qp_ap, bias_ap, v_ap, rds, sd = blk
                if bias_ap is not None:
                    out_ap = banks[sbk][:, j * 128:(j + 2) * 128].rearrange("p (s t) -> p s t", s=2)
                    P.add("pe", lambda e, out_ap=out_ap, bias_ap=bias_ap: e.matmul(
                        out_ap, lhsT=ident[:], rhs=bias_ap, start=False, stop=False, skip_group_check=True),
                        reads=rds + ["ident"], writes=[ps(sbk)])

        def rec_exp(n):
            sbk = B_S[n % len(B_S)]
            ncols = len(loads[n]) * 128
            pi = state["pti"] % NPT
            state["pti"] += 1
            P.add("act", lambda e: e.activation(out=PT[pi][:, 0:ncols], in_=banks[sbk][:, 0:ncols], func=AF.Exp),
                  reads=[], writes=[ps(sbk), ("PT", pi)])
            return pi

        started = set()

        def rec_pv(n, pi):
            for bpos, (gi_, lastb, blk) in enumerate(loads[n]):
                g = groups[gi_]
                h, k_ap, q_ap, bias_ap, v_ap, rds, sd = blk
                ob_ = g["bank"]
                first = (gi_ not in started)
                started.add(gi_)
                out_ap = banks[ob_][:, h * OW:h * OW + 65]
                P.add("pe", lambda e, out_ap=out_ap, v_ap=v_ap, bpos=bpos, first=first: e.matmul(
                    out_ap, lhsT=PT[pi][:, bpos * 128:(bpos + 1) * 128], rhs=v_ap, start=first, stop=False,
                    skip_group_check=True),
                    reads=rds + [("PT", pi)], writes=[ps(ob_)])
                if lastb:
                    g["done"] = True

        g_slot = i % 4
        cur = {"res": [], "wr": []}

        def normalize(src3, H, off, is_win):
            src_res, src_wr = cur["res"], cur["wr"]
            if is_win:
                P.add("dve", lambda e: e.tensor_tensor(out=den[:, 0:H].unsqueeze(2), in0=src3[:, :, 64:65],
                                                       in1=esink[:, 0:H].unsqueeze(2), op=ALU.add),
                      reads=["esink"] + src_res, writes=["den"] + src_wr)
                P.add("dve", lambda e: e.reciprocal(out=recip[:, 0:H], in_=den[:, 0:H]),
                      reads=["den"], writes=["recip"])
            else:
                P.add("dve", lambda e: e.reciprocal(out=recip[:, 0:H].unsqueeze(2), in_=src3[:, :, 64:65]),
                      reads=src_res, writes=["recip"] + src_wr)
            P.add("dve", lambda e: e.tensor_tensor(out=y1[:, 0:H, :], in0=src3[:, :, 0:64],
                                                   in1=recip[:, 0:H].unsqueeze(2).to_broadcast([128, H, 64]), op=ALU.mult),
                  reads=["recip"] + src_res, writes=["y1"] + src_wr)
            P.add("dve", lambda e: e.tensor_tensor(
                out=y_bf[:, off:off + H * 64].rearrange("p (h d) -> p h d", h=H), in0=y1[:, 0:H, :],
                in1=GR[g_slot][:, off:off + H * 64].rearrange("p (h d) -> p h d", h=H), op=ALU.mult),
                reads=["y1", ("GR", g_slot)], writes=["y_bf"])

        finalized = set()

        def finalize_groups():
            for gi_, g in enumerate(groups):
                if not g.get("done") or gi_ in finalized:
                    continue
                if g["name"] == "naC":
                    continue
                finalized.add(gi_)
                H = g["H"]
                if g["name"] == "naD":
                    gc = groups[0]
                    finalized.add(0)

                    def v3(ap):
                        return ap.rearrange("p (h d) -> p h d", h=6)[:, :, 0:65]
                    oc = v3(banks[gc["bank"]][:, 0:6 * OW])
                    od = v3(banks[g["bank"]][:, 0:6 * OW])
                    P.add("dve", lambda e: e.tensor_scalar(out=v3(ob_t[:]), in0=oc, scalar1=flag_sb[:], scalar2=None,
                                                           op0=ALU.mult),
                          reads=["flag"], writes=["ob_t", ps(gc["bank"])])
                    P.add("dve", lambda e: e.scalar_tensor_tensor(out=v3(ob[:]), in0=od, scalar=omf[:], in1=v3(ob_t[:]),
                                                                  op0=ALU.mult, op1=ALU.add),
                          reads=["omf", "ob_t"], writes=["ob", ps(g["bank"])])
                    cur["res"], cur["wr"] = ["ob"], []
                    normalize(ob[:].rearrange("p (h d) -> p h d", h=6), 6, 0, False)
                else:
                    cur["res"], cur["wr"] = [], [ps(g["bank"])]
                    src3 = banks[g["bank"]][:, 0:H * OW].rearrange("p (h d) -> p h d", h=H)
                    off = {"na": 0, "win": 384, "mem": 768}[g["name"]]
                    normalize(src3, H, off, g["name"] == "win")

        NL = len(loads)
        LA = len(B_S)
        for n in range(min(LA, NL)):
            rec_qk(n)
        for n in range(NL):
            pi = rec_exp(n)
            rec_pv(n, pi)
            finalize_groups()
            if n + LA < NL:
                rec_qk(n + LA)

    def tailT(l, i):
        for c in range(8):
            P.add("pe", lambda e, c=c: e.transpose(out=tp_bf[:, c, :], in_=y_bf[:, c * 128:(c + 1) * 128],
                                                   identity=ident[:]),
                  reads=["y_bf", "ident"], writes=[ps(B_T)])
        P.add("dve", lambda e: e.tensor_copy(out=yT[:], in_=tp_bf), reads=[], writes=["yT", ps(B_T)])

    def tailO(l, i, slot):
        xr = xring[slot]
        for half in range(2):
            bo = half
            for k in range(8):
                P.add("pe", lambda e, k=k, bo=bo, half=half: e.matmul(
                    banks[bo][:, :], lhsT=yT[:, k, :], rhs=w_out_sb[:, k, half * 512:(half + 1) * 512],
                    start=(k == 0), stop=(k == 7)),
                    reads=["yT", "w_out"], writes=[ps(bo)])
            P.add("dve", lambda e, bo=bo, half=half: e.tensor_tensor(
                out=xr[:, half * 512:(half + 1) * 512], in0=banks[bo][:, :], in1=xr[:, half * 512:(half + 1) * 512],
                op=ALU.add), reads=[], writes=[ps(bo), ("xr", slot)])
        st = P.add("sp", lambda e: e.dma_start(out=y_out[i * 128:(i + 1) * 128, :], in_=xr[:]),
                   reads=[("xr", slot)], writes=[("hbm_y", i)], dma=f"xs{slot}")
        return st

    stores = []
    for l in range(DEPTH):
        load_weights(l)
        layer_params(l)
        if l == 0:
            mem_load(range(len(mem_tiles)))
            mem_pre()
        mem_kv(l)
        src = x_in if l == 0 else y_out
        slots = {}

        def issue_load(j, src=src, l=l, slots=slots):
            slot = xalloc()
            slots[j] = slot
            load_rows(src[j * 128:(j + 1) * 128, :], slot, [] if l == 0 else [("hbm_y", j)])

        issue_load(0)
        if NT > 1:
            issue_load(1)
        frontA(slots[0])
        frontB(gx_sb, "gx")
        nxt = l + 1 < DEPTH
        for s in range(NT + 4):
            if s + 1 < NT:
                frontA(slots[s + 1])
            has_tail = 0 <= s - 4 < NT
            hk = (lambda: tailT(l, s - 4)) if has_tail else None
            if s < NT:
                proj(l, s, hook=hk)
            elif has_tail:
                hk()
            if has_tail:
                st = tailO(l, s - 4, slots[s - 4])
                if l == DEPTH - 1:
                    stores.append(st)
            if nxt and s == NT:
                mem_load([0, 1])
            if nxt and s == NT + 1:
                mem_load([2, 3])
            if nxt and s == NT + 3:
                mem_load([4, 5])
                mem_pre()
            if s + 1 < NT:
                frontB(gx_sb, "gx", 7)
            if 0 <= s - 3 < NT:
                attn(l, s - 3)
            if s < 3:
                layer_bias(l, [2 * s, 2 * s + 1])
            if s + 2 < NT:
                issue_load(s + 2)
    fin = Op()
    fin.eng = "sp"
    fin.fn = None
    fin.is_dma = False
    fin.sem = None
    fin.signal = False
    fin.val = 0
    fin.deps = stores[-XS:] if len(stores) >= XS else stores
    P.ops["sp"].append(fin)
    P.all.append(fin)

    dma_keys = P.finalize()
    sems = {}
    for e_ in Prog.ENGS:
        sems[e_] = es.enter_context(nc.semaphore(f"s_{e_}"))
    for k_ in dma_keys:
        sems[k_] = es.enter_context(nc.semaphore(f"d_{k_}"))
    block = es.enter_context(nc.Block())

    @block.tensor
    def _(eng):
        P.emit_engine("pe", eng, sems)

    @block.scalar
    def _(eng):
        P.emit_engine("act", eng, sems)

    @block.vector
    def _(eng):
        P.emit_engine("dve", eng, sems)

    @block.gpsimd
    def _(eng):
        P.emit_engine("pool", eng, sems)

    @block.sync
    def _(eng):
        P.emit_engine("sp", eng, sems)

    es.close()
    return nc


def make_consts():
    k = np.arange(128)
    kr, kc = k // 64, k % 64
    qr, qc = k // 64, k % 64
    cs = np.clip(qc - 8, 0, 48)
    colok = (kc[:, None] >= cs[None, :]) & (kc[:, None] < cs[None, :] + 16)
    m2ok = kr[:, None] >= qr[None, :]
    p2ok = (kr[:, None] == 0) & (qr[None, :] == 1)
    cp = np.zeros((128, NCP), np.float32)
    cp[:, C_CM:C_CM + 128] = np.where(colok, 0.0, NEG)
    cp[:, C_CM_M2:C_CM_M2 + 128] = np.where(colok & m2ok, 0.0, NEG)
    cp[:, C_CM_P2:C_CM_P2 + 128] = np.where(colok & p2ok, 0.0, NEG)
    cp[:, C_QSCALE:C_QSCALE + 6] = np.array([0.125, 1.0, 0.125, 1.0, 0.125, 1.0], np.float32)[None, :]
    cp[:, C_MASKAB] = (k < 64).astype(np.float32)
    cp[:, C_MASKAB + 1] = (k >= 64).astype(np.float32)
    ca = np.zeros((128, NCA), np.float32)
    ca[:, 0:128] = np.eye(128, dtype=np.float32)
    blk = (k[:, None] // 64) == (k[None, :] // 64)
    ca[:, 128:256] = np.where(blk, 1.0 / 64.0, 0.0)
    cb = np.zeros((128, NCB), np.float32)
    for di, dl in enumerate((-1, 0, 1)):
        dist = (k[None, :] - k[:, None] - 128 * dl).astype(np.float32)
        ad = np.abs(dist)
        cb[:, di * 128:(di + 1) * 128] = ad
        cb[:, 384 + di * 128:384 + (di + 1) * 128] = np.where(ad <= 128, 0.0, NEG)
    return cp, ca, cb


def expand_rpb(rpb):
    L = rpb.shape[0]
    k = np.arange(128)
    kr, kc = k // 64, k % 64
    out = np.empty((L, 6, 128, 7, 128), np.float32)
    colidx = np.clip(kc[:, None] - kc[None, :] + 15, 0, 30)
    for di in range(7):
        rowidx = 2 * (di - 3) + kr[:, None] - kr[None, :] + 7
        rowidx = np.clip(rowidx, 0, 14)
        out[:, :, :, di, :] = rpb[:, :, rowidx, colidx]
    return np.ascontiguousarray(out.reshape(L * 6, 128, 7 * 128))


def chunk_cols(g):
    L = g.shape[0]
    return np.ascontiguousarray(g.reshape(L, 8, 128).transpose(0, 2, 1))


_CACHE = {}


def _get_program(depth, seg, nseg):
    key = (depth, seg, nseg)
    if key not in _CACHE:
        _CACHE[key] = build_program(depth, seg, nseg)
    return _CACHE[key]


def run_cores(core_inputs, shared, depth, seg, nseg):
    nc = _get_program(depth, seg, nseg)
    in_maps = []
    for ci in core_inputs:
        m = dict(shared)
        m.update(ci)
        in_maps.append(m)
    res = run_bass_kernel_spmd(nc, in_maps, core_ids=list(range(len(in_maps))))
    return [r["y"] for r in res.results]


def make_shared(norm_g, w_in, q_norm_g, k_norm_g, rpb, sink, mem_norm_g, w_mem_kv, w_out):
    L = norm_g.shape[0]
    gqk = np.empty((L, 128, 6), np.float32)
    for gi in range(3):
        gqk[:, :, 2 * gi] = np.concatenate([q_norm_g[:, gi, :], q_norm_g[:, gi, :]], axis=1)
        gqk[:, :, 2 * gi + 1] = np.concatenate([k_norm_g[:, gi, :], k_norm_g[:, gi, :]], axis=1)
    sinkb = np.ascontiguousarray(np.broadcast_to(sink[:, None, :], (L, 128, 6))).astype(np.float32)
    return {
        "w_in": np.ascontiguousarray(w_in, dtype=np.float32),
        "w_out": np.ascontiguousarray(w_out, dtype=np.float32),
        "w_mem": np.ascontiguousarray(w_mem_kv, dtype=np.float32),
        "gx": chunk_cols(np.asarray(norm_g, np.float32)),
        "gm": chunk_cols(np.asarray(mem_norm_g, np.float32)),
        "gqk": gqk,
        "sinkb": sinkb,
        "rpbT": expand_rpb(np.asarray(rpb, np.float32)),
        "consts": make_consts()[0],
        "constsA": make_consts()[1],
        "constsB": make_consts()[2],
    }


def kernel(x_prompt, x_sample, mem_prompt, mem_sample, norm_g, w_in, q_norm_g, k_norm_g, rpb, sink,
           mem_norm_g, w_mem_kv, w_out):
    x_prompt = np.asarray(x_prompt, np.float32)
    x_sample = np.asarray(x_sample, np.float32)
    mem_prompt = np.asarray(mem_prompt, np.float32)
    mem_sample = np.asarray(mem_sample, np.float32)
    shared = make_shared(np.asarray(norm_g, np.float32), np.asarray(w_in, np.float32), np.asarray(q_norm_g, np.float32),
                         np.asarray(k_norm_g, np.float32), np.asarray(rpb, np.float32), np.asarray(sink, np.float32),
                         np.asarray(mem_norm_g, np.float32), np.asarray(w_mem_kv, np.float32),
                         np.asarray(w_out, np.float32))
    cores = []
    for c in range(8):
        if c < 4:
            xs = np.concatenate([x_prompt[c], x_sample[c]], axis=0)
            mm = np.concatenate([mem_prompt[c], mem_prompt[c], mem_sample[c]], axis=0)
            f = 1.0
        else:
            ids = [4 + 3 * (c - 4) + t for t in range(3)]
            xs = np.concatenate([x_sample[t] for t in ids], axis=0)
            mm = np.concatenate([mem_sample[t] for t in ids], axis=0)
            f = 0.0
        cores.append({"x": np.ascontiguousarray(xs), "mem": np.ascontiguousarray(mm),
                      "flag": np.full((128, 1), f, np.float32)})
    ys = run_cores(cores, shared, 4, 16, 3)
    y_prompt = np.empty_like(x_prompt)
    y_sample = np.empty_like(x_sample)
    for c in range(8):
        y = ys[c]
        if c < 4:
            y_prompt[c] = y[0:4096]
            y_sample[c] = y[4096:6144]
        else:
            for t in range(3):
                y_sample[4 + 3 * (c - 4) + t] = y[t * 2048:(t + 1) * 2048]
    return (y_prompt, y_sample)
```
